# Optimizing a Trainium2 kernel written in Bass

```python
import math
import jax, jax.numpy as jnp
from jax import lax
import numpy as np

D_MODEL = 2048
BATCH = 2
SEQ = 8192
DEPTH = 1

N_META = 16
ATTN_HEADS = 8
ATTN_HEAD_DIM = 128
D_ATTN = ATTN_HEADS * ATTN_HEAD_DIM
D_SSM = D_MODEL // 2
SSM_GROUP = 16
SSM_GROUPS = D_SSM // SSM_GROUP
SSM_STATE = 64
D_FF = 5632
CONV_WIDTH = 3
Q_BLOCK = 128
EPS = 1e-6
IN_SPLITS = [D_ATTN, D_ATTN, D_ATTN, ATTN_HEADS, D_SSM, D_MODEL, D_MODEL]
N_IN = sum(IN_SPLITS)
IN_OFFSETS = [int(o) for o in np.cumsum(IN_SPLITS)[:-1]]

kernel_name = "hybrid_s5_forgetting_attn_convffn"


def rmsnorm(x, g):
    xf = x.astype(jnp.float32)
    y = xf * lax.rsqrt(jnp.mean(xf * xf, axis=-1, keepdims=True) + EPS)
    return (y * g.astype(jnp.float32)).astype(x.dtype)


def _fox_block(qb, Fq, qpos, k, v, Fk, kpos):
    s = jnp.einsum('bqhd,bkhd->bhqk', qb, k, preferred_element_type=jnp.float32) * (ATTN_HEAD_DIM ** -0.5)
    s = s + jnp.transpose(Fq, (0, 2, 1))[..., None] - jnp.transpose(Fk, (0, 2, 1))[:, :, None, :]
    mask = kpos[None, :] <= qpos[:, None]
    s = jnp.where(mask[None, None], s, -jnp.inf)
    p = jax.nn.softmax(s, axis=-1)
    return jnp.einsum('bhqk,bkhd->bqhd', p.astype(v.dtype), v)


def forgetting_attention(q, k, v, log_f):
    b, L, H, hd = q.shape
    F = jnp.cumsum(log_f, axis=1)
    pos = jnp.arange(L)
    out_meta = _fox_block(q[:, :N_META], F[:, :N_META], pos[:N_META],
                          k[:, :N_META], v[:, :N_META], F[:, :N_META], pos[:N_META])
    n_blk = (L - N_META) // Q_BLOCK
    qr = q[:, N_META:].reshape(b, n_blk, Q_BLOCK, H, hd).transpose(1, 0, 2, 3, 4)
    Fr = F[:, N_META:].reshape(b, n_blk, Q_BLOCK, H).transpose(1, 0, 2, 3)
    qpos = (N_META + jnp.arange(L - N_META)).reshape(n_blk, Q_BLOCK)
    out_r = lax.map(lambda a: _fox_block(a[0], a[1], a[2], k, v, F, pos), (qr, Fr, qpos))
    out_r = out_r.transpose(1, 0, 2, 3, 4).reshape(b, L - N_META, H, hd)
    return jnp.concatenate([out_meta, out_r], axis=1)


def s5_ssm(u, lam_re, lam_im, log_dt, b_re, b_im, c_re, c_im, d_skip):
    bsz, L, _ = u.shape
    f32 = jnp.float32
    uf = u.astype(f32).reshape(bsz, L, SSM_GROUPS, SSM_GROUP)
    dt = jnp.exp(log_dt.astype(f32))[:, None]
    lr = lam_re.astype(f32)
    li = lam_im.astype(f32)
    mag = jnp.exp(lr * dt)
    a_re = mag * jnp.cos(li * dt)
    a_im = mag * jnp.sin(li * dt)
    den = lr * lr + li * li
    nr = a_re - 1.0
    z_re = (nr * lr + a_im * li) / den
    z_im = (a_im * lr - nr * li) / den
    br = b_re.astype(f32)
    bi = b_im.astype(f32)
    bb_re = z_re[..., None] * br - z_im[..., None] * bi
    bb_im = z_re[..., None] * bi + z_im[..., None] * br
    bu_re = jnp.einsum('gpc,blgc->blgp', bb_re, uf)
    bu_im = jnp.einsum('gpc,blgc->blgp', bb_im, uf)
    at_re = jnp.broadcast_to(a_re, (1, L, SSM_GROUPS, SSM_STATE))
    at_im = jnp.broadcast_to(a_im, (1, L, SSM_GROUPS, SSM_STATE))

    def combine(e1, e2):
        ar1, ai1, br1, bi1 = e1
        ar2, ai2, br2, bi2 = e2
        return (ar2 * ar1 - ai2 * ai1,
                ar2 * ai1 + ai2 * ar1,
                ar2 * br1 - ai2 * bi1 + br2,
                ar2 * bi1 + ai2 * br1 + bi2)

    _, _, h_re, h_im = lax.associative_scan(combine, (at_re, at_im, bu_re, bu_im), axis=1)
    y = (jnp.einsum('gcp,blgp->blgc', c_re.astype(f32), h_re)
         - jnp.einsum('gcp,blgp->blgc', c_im.astype(f32), h_im))
    y = y.reshape(bsz, L, D_SSM) + d_skip.astype(f32) * u.astype(f32)
    return y.astype(u.dtype)


def conv_ffn(x, w_up, conv_w, conv_b, w_down):
    gu = x @ w_up
    g, u = jnp.split(gu, 2, axis=-1)
    L = g.shape[1]
    gp = jnp.pad(g, ((0, 0), (CONV_WIDTH - 1, 0), (0, 0)))
    gc = conv_b + conv_w[0] * gp[:, 0:L]
    for j in range(1, CONV_WIDTH):
        gc = gc + conv_w[j] * gp[:, j:j + L]
    return (jax.nn.silu(gc) * u) @ w_down


def mixer(n, w_in, b_f, lam_re, lam_im, log_dt, b_re, b_im, c_re, c_im, d_skip, w_glu, w_attn_o, w_out):
    bsz, L, _ = n.shape
    z = n @ w_in
    q, k, v, f, u, ga, gb = jnp.split(z, IN_OFFSETS, axis=-1)
    q = q.reshape(bsz, L, ATTN_HEADS, ATTN_HEAD_DIM)
    k = k.reshape(bsz, L, ATTN_HEADS, ATTN_HEAD_DIM)
    v = v.reshape(bsz, L, ATTN_HEADS, ATTN_HEAD_DIM)
    log_f = jax.nn.log_sigmoid(f.astype(jnp.float32) + b_f.astype(jnp.float32))
    attn = forgetting_attention(q, k, v, log_f).reshape(bsz, L, D_ATTN) @ w_attn_o
    y = s5_ssm(u, lam_re, lam_im, log_dt, b_re, b_im, c_re, c_im, d_skip)
    ya, yb = jnp.split(jax.nn.gelu(y) @ w_glu, 2, axis=-1)
    ssm_out = ya * jax.nn.sigmoid(yb)
    merged = jax.nn.sigmoid(ga) * ssm_out + jax.nn.sigmoid(gb) * attn
    return merged @ w_out


def setup_inputs(seed: int = 0) -> dict:
    key = jax.random.key(seed)
    ks = jax.random.split(key, 24)
    nrm = lambda k, s, sc: jax.random.normal(k, s, jnp.float32) * sc
    Dp = DEPTH
    n_idx = jnp.arange(SSM_STATE, dtype=jnp.float32)
    return {
        "x": nrm(ks[0], (BATCH, SEQ, D_MODEL), 1.0),
        "meta": nrm(ks[1], (N_META, D_MODEL), 1.0),
        "g_mix": 1.0 + nrm(ks[2], (Dp, D_MODEL), 0.01),
        "w_in": nrm(ks[3], (Dp, D_MODEL, N_IN), D_MODEL ** -0.5),
        "b_f": jax.random.uniform(ks[4], (Dp, ATTN_HEADS), jnp.float32, 1.0, 6.0),
        "lam_re": -0.5 + nrm(ks[5], (Dp, SSM_GROUPS, SSM_STATE), 0.01),
        "lam_im": math.pi * n_idx + nrm(ks[6], (Dp, SSM_GROUPS, SSM_STATE), 0.01),
        "log_dt": jax.random.uniform(ks[7], (Dp, SSM_GROUPS), jnp.float32, math.log(1e-3), math.log(1e-1)),
        "b_re": nrm(ks[8], (Dp, SSM_GROUPS, SSM_STATE, SSM_GROUP), (2 * SSM_GROUP) ** -0.5),
        "b_im": nrm(ks[9], (Dp, SSM_GROUPS, SSM_STATE, SSM_GROUP), (2 * SSM_GROUP) ** -0.5),
        "c_re": nrm(ks[10], (Dp, SSM_GROUPS, SSM_GROUP, SSM_STATE), (2 * SSM_STATE) ** -0.5),
        "c_im": nrm(ks[11], (Dp, SSM_GROUPS, SSM_GROUP, SSM_STATE), (2 * SSM_STATE) ** -0.5),
        "d_skip": nrm(ks[12], (Dp, D_SSM), 1.0),
        "w_glu": nrm(ks[13], (Dp, D_SSM, 2 * D_MODEL), D_SSM ** -0.5),
        "w_attn_o": nrm(ks[14], (Dp, D_ATTN, D_MODEL), D_ATTN ** -0.5),
        "w_out": nrm(ks[15], (Dp, D_MODEL, D_MODEL), D_MODEL ** -0.5),
        "g_ffn": 1.0 + nrm(ks[16], (Dp, D_MODEL), 0.01),
        "w_up": nrm(ks[17], (Dp, D_MODEL, 2 * D_FF), D_MODEL ** -0.5),
        "conv_w": nrm(ks[18], (Dp, CONV_WIDTH, D_FF), CONV_WIDTH ** -0.5),
        "conv_b": nrm(ks[19], (Dp, D_FF), 0.01),
        "w_down": nrm(ks[20], (Dp, D_FF, D_MODEL), D_FF ** -0.5),
        "g_final": 1.0 + nrm(ks[21], (D_MODEL,), 0.01),
    }


def reference(x, meta, g_mix, w_in, b_f, lam_re, lam_im, log_dt, b_re, b_im, c_re, c_im, d_skip,
              w_glu, w_attn_o, w_out, g_ffn, w_up, conv_w, conv_b, w_down, g_final):
    bsz = x.shape[0]
    m = jnp.broadcast_to(meta.astype(x.dtype)[None], (bsz, N_META, D_MODEL))
    h = jnp.concatenate([m, x], axis=1)
    for l in range(DEPTH):
        n = rmsnorm(h, g_mix[l])
        h = h + mixer(n, w_in[l], b_f[l], lam_re[l], lam_im[l], log_dt[l], b_re[l], b_im[l],
                      c_re[l], c_im[l], d_skip[l], w_glu[l], w_attn_o[l], w_out[l])
        n2 = rmsnorm(h, g_ffn[l])
        h = h + conv_ffn(n2, w_up[l], conv_w[l], conv_b[l], w_down[l])
    return rmsnorm(h, g_final)[:, N_META:]
```

```python
from contextlib import ExitStack
import numpy as np
import ml_dtypes
import concourse.bass as bass
import concourse.mybir as mybir
from concourse.bass_utils import run_bass_kernel_spmd

F32 = mybir.dt.float32
BF16 = mybir.dt.bfloat16
AF = mybir.ActivationFunctionType
ALU = mybir.AluOpType

D = 2048
NT_ALL = 65
R_ALL = NT_ALL * 128
NB = 16
BW = 136
R_OWN = NB * BW
H = 8
DFF = 5632
NEG = -30000.0
OFF_Q, OFF_K, OFF_V, OFF_F, OFF_U, OFF_GA, OFF_GB = 0, 1024, 2048, 3072, 3080, 4104, 6152
N_IN = 8200
DEBUG = False


class Buf:
    __slots__ = ("w", "r")

    def __init__(self):
        self.w = None
        self.r = {}


def bufs(n):
    return [Buf() for _ in range(n)]


class Sched:
    NDS = 8

    def __init__(self, nc):
        self.nc = nc
        self.e = dict(pe=nc.tensor, act=nc.scalar, dve=nc.vector, pool=nc.gpsimd, sp=nc.sync)
        self.csem = {}
        self.ccnt = {}
        self.nsem = 0
        for k in self.e:
            self._new_csem(k)
        self.waited = {k: {} for k in self.e}
        self.dsems = {k: [] for k in self.e}
        self.dnext = {k: 0 for k in self.e}
        self.ninstr = 0

    def _alloc(self, nm):
        self.nsem += 1
        return self.nc.alloc_semaphore(name=f"{nm}_{self.nsem}")

    def _new_csem(self, k):
        self.csem[k] = self._alloc("c" + k)
        self.ccnt[k] = 0

    def _deps(self, eng, r, w):
        evs = []
        for b in r:
            if b.w is not None:
                evs.append(b.w)
        for b in w:
            if b.w is not None:
                evs.append(b.w)
            for key, ev in b.r.items():
                evs.append(ev)
        return evs

    def _wait(self, eng, evs):
        best = {}
        for ev in evs:
            h, v, src = ev
            if eng == "pe" and src == "pe":
                continue
            if best.get(h.num, (None, 0))[1] < v:
                best[h.num] = (h, v)
        wd = self.waited[eng]
        for num, (h, v) in best.items():
            if wd.get(num, 0) >= v:
                continue
            self.e[eng].wait_ge(h, v)
            self.ninstr += 1
            wd[num] = v

    def op(self, eng, fn, r=(), w=()):
        self._wait(eng, self._deps(eng, r, w))
        ins = fn(self.e[eng])
        self.ninstr += 1
        if self.ccnt[eng] >= 20000:
            self._new_csem(eng)
        self.ccnt[eng] += 1
        h = self.csem[eng]
        ins.then_inc(h, 1)
        ev = (h, self.ccnt[eng], eng)
        for b in w:
            b.w = ev
            b.r = {}
        for b in r:
            b.r[eng] = ev
        return ev

    def dma(self, out, in_, r=(), w=(), eng="sp"):
        evs = self._deps("dma", r, w)
        pool = self.dsems[eng]
        if len(pool) < self.NDS:
            pool.append([self._alloc("d" + eng), 0])
        idx = self.dnext[eng] % self.NDS
        self.dnext[eng] += 1
        slot = pool[idx]
        if slot[1] > 0:
            evs.append((slot[0], slot[1], "dma"))
        self._wait(eng, evs)
        if slot[1] >= 30000:
            slot[0] = self._alloc("d" + eng)
            slot[1] = 0
        ins = self.e[eng].dma_start(out=out, in_=in_)
        self.ninstr += 1
        slot[1] += 16
        ins.then_inc(slot[0], 16)
        ev = (slot[0], slot[1], "dma")
        for b in w:
            b.w = ev
            b.r = {}
        for b in r:
            b.r[("d", eng, idx)] = ev
        return ev

    def barrier(self):
        evs = []
        for k in self.e:
            if self.ccnt[k] > 0:
                evs.append((self.csem[k], self.ccnt[k], k))
        for eng, pool in self.dsems.items():
            for slot in pool:
                if slot[1] > 0:
                    evs.append((slot[0], slot[1], "dma"))
        for k in self.e:
            self._wait(k, [ev for ev in evs if ev[2] != k])

    def finish(self, evs):
        self._wait("sp", evs)


class PhaseStack(ExitStack):
    def __init__(self, sched):
        super().__init__()
        self._sched = sched

    def __exit__(self, *a):
        r = super().__exit__(*a)
        self._sched.barrier()
        return r

    def close(self):
        super().close()
        self._sched.barrier()


def own_row(i, q=0):
    return 128 * (4 * i + 3) + 8 + q


def build(stage="full"):
    nc = bass.Bass("TRN2", target_bir_lowering=False)
    S = Sched(nc)

    def din(name, shape, dt=F32):
        return nc.dram_tensor(name, list(shape), dt, kind="ExternalInput").ap()

    def dscr(name, shape, dt):
        return nc.dram_tensor(name, list(shape), dt, kind="Internal").ap()

    hall = din("hall", [R_ALL, D])
    kbneg = din("kbneg", [128, NT_ALL])
    w_in = din("w_in", [D, N_IN])
    g_mix = din("g_mix", [D])
    b_f = din("b_f", [H])
    c_ident = din("c_ident", [128, 128])
    c_tri = din("c_tri", [128, 128])
    c_sel72 = din("c_sel72", [128, 128])
    c_mask = din("c_mask", [128, 2 * BW])
    lam_re = din("lam_re", [64, 64])
    lam_im = din("lam_im", [64, 64])
    log_dt = din("log_dt", [64])
    b_re = din("b_re", [64, 64, 16])
    b_im = din("b_im", [64, 64, 16])
    c_re = din("c_re", [64, 16, 64])
    c_im = din("c_im", [64, 16, 64])
    d_skip = din("d_skip", [1024])
    c_kk = din("c_kk", [24])
    w_glu = din("w_glu", [1024, 4096])
    w_attn_o = din("w_attn_o", [1024, D])
    w_out = din("w_out", [D, D])
    g_ffn = din("g_ffn", [D])
    w_up = din("w_up", [D, 2 * DFF])
    conv_w = din("conv_w", [3, DFF])
    conv_b = din("conv_b", [DFF])
    w_down = din("w_down", [DFF, D])
    g_final = din("g_final", [D])
    c_rowmask8 = din("c_rowmask8", [128, 8])
    c_cmask = din("c_cmask", [128, 4 * 128])
    c_bdmask = din("c_bdmask", [128, 128])

    KT = dscr("KT", [H, 128, R_ALL], BF16)
    VV = dscr("VV", [NT_ALL, 128, 1024], BF16)
    UT = dscr("UT", [8, 128, R_ALL], BF16)
    SGA = dscr("SGA", [16, 128, R_OWN], BF16)
    SGB = dscr("SGB", [16, 128, R_OWN], BF16)
    AT = dscr("AT", [H, 128, R_OWN], BF16)
    b_KTd = [bufs(17) for _ in range(H)]
    b_UTd = [bufs(17) for _ in range(8)]
    b_VVd = bufs(NT_ALL)
    b_SGAd = bufs(16)
    b_SGBd = bufs(16)
    b_ATd = bufs(H)

    outs = {}

    def dout(name, shape, dt=F32):
        t = nc.dram_tensor(name, list(shape), dt, kind="ExternalOutput").ap()
        outs[name] = t
        return t

    final_evs = []
    es_top = ExitStack()
    es_top.enter_context(nc.allow_non_contiguous_dma(reason="small strided parameter / layout DMAs"))

    uid = [0]

    def sb(es, name, shape, dt):
        uid[0] += 1
        return es.enter_context(nc.sbuf_tensor(f"{name}_{uid[0]}", list(shape), dt))

    PS = [es_top.enter_context(nc.psum_tensor(f"ps{i}", [128, 512], F32)) for i in range(8)]
    PSB = bufs(8)

    ident_f = sb(es_top, "ident_f", [128, 128], F32)
    ident_b = sb(es_top, "ident_b", [128, 128], BF16)
    ones_b = sb(es_top, "ones_b", [128, 128], BF16)
    ones_f = sb(es_top, "ones_f", [128, 128], F32)
    tri_f = sb(es_top, "tri_f", [128, 128], F32)
    sel72_f = sb(es_top, "sel72_f", [128, 128], F32)
    mask_f = sb(es_top, "mask_f", [128, 2 * BW], F32)
    mask_b = sb(es_top, "mask_b", [128, 2 * BW], BF16)
    G1 = sb(es_top, "G1", [128, 16], F32)
    cst = sb(es_top, "cst", [128, 8], F32)
    NBt = sb(es_top, "NBt", [128, NT_ALL, H], F32)
    NREF = sb(es_top, "NREF", [128, NB, H], F32)
    b_const = Buf()
    b_NB = Buf()
    b_NREF = Buf()
    b_QT = bufs(H)

    S.dma(ident_f[:], c_ident[:, :], w=[b_const])
    S.dma(tri_f[:], c_tri[:, :], w=[b_const])
    S.dma(sel72_f[:], c_sel72[:, :], w=[b_const])
    S.dma(mask_f[:], c_mask[:, :], w=[b_const])
    S.dma(G1[:], g_mix.rearrange("(t p) -> p t", p=128), w=[b_const])
    S.op("dve", lambda e: e.tensor_copy(out=ident_b[:], in_=ident_f[:]), r=[b_const], w=[b_const])
    S.op("dve", lambda e: e.tensor_copy(out=mask_b[:], in_=mask_f[:]), r=[b_const], w=[b_const])
    S.op("pool", lambda e: e.memset(ones_b[:], 1.0), w=[b_const])
    S.op("pool", lambda e: e.memset(ones_f[:], 1.0), w=[b_const])
    S.op("pool", lambda e: e.memset(cst[:, 0:1], 1e-6), w=[b_const])
    S.op("pool", lambda e: e.memset(cst[:, 1:2], -0.5), w=[b_const])
    S.op("pool", lambda e: e.memset(cst[:, 2:3], 1.0), w=[b_const])

    def norm_transpose(es_bufs, xt, b_xt, nrows, dst_fn, b_dst, ps_pair):
        junk, ssq, ms, rstd, xs, b_tmp, b_xs = es_bufs
        S.op("act", lambda e: e.activation(out=junk[0:nrows, :], in_=xt[0:nrows, :], func=AF.Square),
             r=[b_xt], w=[b_tmp])
        w_ = D // 2
        while w_ >= 1:
            S.op("dve", lambda e, w_=w_: e.tensor_tensor(out=junk[0:nrows, 0:w_], in0=junk[0:nrows, 0:w_],
                                                         in1=junk[0:nrows, w_:2 * w_], op=ALU.add),
                 r=[b_tmp], w=[b_tmp])
            w_ //= 2
        ssq = junk
        S.op("act", lambda e: e.activation(out=ms[0:nrows, :], in_=ssq[0:nrows, 0:1], func=AF.Sqrt, scale=1.0 / D,
                                           bias=cst[0:nrows, 0:1]), r=[b_tmp, b_const], w=[b_tmp])
        S.op("dve", lambda e: e.reciprocal(out=rstd[0:nrows, :], in_=ms[0:nrows, :]), r=[b_tmp], w=[b_tmp])
        S.op("dve", lambda e: e.tensor_scalar(out=xs[0:nrows, :], in0=xt[0:nrows, :], scalar1=rstd[0:nrows, 0:1],
                                              scalar2=None, op0=ALU.mult), r=[b_tmp, b_xt], w=[b_xs])
        for half in range(2):
            pi = ps_pair[half]
            pv = PS[pi][:].bitcast(BF16)
            for j in range(8):
                dt_ = half * 8 + j
                S.op("pe", lambda e, dt_=dt_, j=j: e.transpose(out=pv[:, j * 128: j * 128 + nrows],
                                                               in_=xs[0:nrows, dt_ * 128:(dt_ + 1) * 128],
                                                               identity=ident_b[0:nrows, 0:nrows]),
                     r=[b_xs, b_const], w=[PSB[pi]])
            src = pv.rearrange("p (j c) -> p j c", c=128)[:, :, 0:nrows]
            eng = "act" if half == 0 else "dve"
            if eng == "act":
                S.op("act", lambda e, src=src, half=half: e.copy(out=dst_fn(half), in_=src), r=[PSB[pi]], w=[b_dst[half]])
            else:
                S.op("dve", lambda e, src=src, half=half: e.tensor_copy(out=dst_fn(half), in_=src), r=[PSB[pi]], w=[b_dst[half]])

    def cast_scaled(eng, out, in_, scale_ap, r, w):
        if eng == "act":
            return S.op("act", lambda e: e.activation(out=out, in_=in_, func=AF.Identity, scale=scale_ap), r=r, w=w)
        return S.op("dve", lambda e: e.tensor_scalar(out=out, in0=in_, scalar1=scale_ap, scalar2=None, op0=ALU.mult), r=r, w=w)

    def cast_plain(eng, out, in_, r, w):
        if eng == "act":
            return S.op("act", lambda e: e.copy(out=out, in_=in_), r=r, w=w)
        return S.op("dve", lambda e: e.tensor_copy(out=out, in_=in_), r=r, w=w)

    with PhaseStack(S) as es:
        WK = sb(es, "WKVU", [128, 16, 3080], BF16)
        b_WK = bufs(16)
        stg = [sb(es, f"wstg{i}", [128, 1024], F32) for i in range(2)]
        b_stg = bufs(2)
        fstg = sb(es, "fstg", [128, 16, 8], F32)
        b_fstg = Buf()
        S.dma(fstg[:], w_in[:, OFF_F:OFF_F + 8].rearrange("(t p) c -> p t c", p=128), w=[b_fstg])
        k = 0
        for dt_ in range(16):
            for pi, off in enumerate((OFF_K, OFF_V, OFF_U)):
                sl = k % 2
                k += 1
                S.dma(stg[sl][:], w_in[dt_ * 128:(dt_ + 1) * 128, off:off + 1024], w=[b_stg[sl]])
                cast_scaled("act" if k % 2 else "dve", WK[:, dt_, pi * 1024:(pi + 1) * 1024], stg[sl][:],
                            G1[:, dt_:dt_ + 1], [b_stg[sl], b_const], [b_WK[dt_]])
            cast_scaled("dve", WK[:, dt_, 3072:3080], fstg[:, dt_, :], G1[:, dt_:dt_ + 1], [b_fstg, b_const], [b_WK[dt_]])

        xts = [sb(es, f"xt{i}", [128, D], F32) for i in range(2)]
        b_xts = bufs(2)
        junk = sb(es, "junk", [128, D], F32)
        ssq = sb(es, "ssq", [128, 1], F32)
        ms = sb(es, "ms", [128, 1], F32)
        rstd = sb(es, "rstd", [128, 1], F32)
        xs = sb(es, "xs", [128, D], BF16)
        nb_ = (junk, ssq, ms, rstd, xs, Buf(), Buf())
        nT = [sb(es, f"nT{i}", [128, 16, 512], BF16) for i in range(2)]
        b_nT = [[bufs(2) for _ in range(4)] for _ in range(2)]
        kst = [sb(es, f"kst{i}", [128, 512], BF16) for i in range(2)]
        b_kst = bufs(2)
        vst = [sb(es, f"vst{i}", [128, 1024], BF16) for i in range(2)]
        b_vst = bufs(2)
        Fraw = sb(es, "Fraw", [128, NT_ALL, H], F32)
        b_Fraw = Buf()

        ngroups = (NT_ALL + 3) // 4
        kcount = 0
        vcount = 0
        gcount = 0
        for gi in range(ngroups):
            tiles = list(range(gi * 4, min(gi * 4 + 4, NT_ALL)))
            gb = gi % 2
            N = 128 * len(tiles)
            for tc, t in enumerate(tiles):
                sl = t % 2
                S.dma(xts[sl][:], hall[t * 128:(t + 1) * 128, :], w=[b_xts[sl]])
                norm_transpose(nb_, xts[sl], b_xts[sl], 128,
                               lambda half, tc=tc, gb=gb: nT[gb][:, half * 8:(half + 1) * 8, tc * 128:(tc + 1) * 128],
                               b_nT[gb][tc], (0, 1))
            rdeps = [b for tc in range(len(tiles)) for b in b_nT[gb][tc]]
            for which, coff, dst in (("k", 0, KT), ("u", 2048, UT)):
                for c in range(8):
                    pi = 2 + (gcount % 3)
                    gcount += 1
                    for dt_ in range(16):
                        S.op("pe", lambda e, dt_=dt_, c=c, coff=coff, pi=pi: e.matmul(
                            PS[pi][:, 0:N], lhsT=WK[:, dt_, coff + c * 128: coff + (c + 1) * 128],
                            rhs=nT[gb][:, dt_, 0:N], start=(dt_ == 0), stop=(dt_ == 15)),
                            r=rdeps + [b_WK[dt_]], w=[PSB[pi]])
                    ks = kcount % 2
                    kcount += 1
                    if kcount % 2 == 0:
                        S.op("act", lambda e, ks=ks, pi=pi: e.copy(out=kst[ks][:, 0:N], in_=PS[pi][:, 0:N]),
                             r=[PSB[pi]], w=[b_kst[ks]])
                    else:
                        S.op("dve", lambda e, ks=ks, pi=pi: e.tensor_copy(out=kst[ks][:, 0:N], in_=PS[pi][:, 0:N]),
                             r=[PSB[pi]], w=[b_kst[ks]])
                    S.dma(dst[c, :, gi * 512: gi * 512 + N], kst[ks][:, 0:N], r=[b_kst[ks]],
                          w=[(b_KTd if which == "k" else b_UTd)[c][gi]], eng="pool")
            for tc, t in enumerate(tiles):
                vs = vcount % 2
                vcount += 1
                for half in range(2):
                    pi = 2 + (gcount % 3)
                    gcount += 1
                    for dt_ in range(16):
                        S.op("pe", lambda e, dt_=dt_, tc=tc, half=half, pi=pi: e.matmul(
                            PS[pi][:, :], lhsT=nT[gb][:, dt_, tc * 128:(tc + 1) * 128],
                            rhs=WK[:, dt_, 1024 + half * 512: 1024 + (half + 1) * 512],
                            start=(dt_ == 0), stop=(dt_ == 15)),
                            r=b_nT[gb][tc] + [b_WK[dt_]], w=[PSB[pi]])
                    if half == 0:
                        S.op("act", lambda e, vs=vs, pi=pi, half=half: e.copy(
                            out=vst[vs][:, half * 512:(half + 1) * 512], in_=PS[pi][:, :]),
                            r=[PSB[pi]], w=[b_vst[vs]])
                    else:
                        S.op("dve", lambda e, vs=vs, pi=pi, half=half: e.tensor_copy(
                            out=vst[vs][:, half * 512:(half + 1) * 512], in_=PS[pi][:, :]),
                            r=[PSB[pi]], w=[b_vst[vs]])
                S.dma(VV[t, :, :], vst[vs][:], r=[b_vst[vs]], w=[b_VVd[t]], eng="pool")
                pi = 5
                for dt_ in range(16):
                    S.op("pe", lambda e, dt_=dt_, tc=tc, pi=pi: e.matmul(
                        PS[pi][:, 0:8], lhsT=nT[gb][:, dt_, tc * 128:(tc + 1) * 128], rhs=WK[:, dt_, 3072:3080],
                        start=(dt_ == 0), stop=(dt_ == 15)), r=b_nT[gb][tc] + [b_WK[dt_]], w=[PSB[pi]])
                S.op("dve", lambda e, t=t, pi=pi: e.tensor_copy(out=Fraw[:, t, :], in_=PS[pi][:, 0:8]),
                     r=[PSB[pi]], w=[b_Fraw])

        bfB = sb(es, "bfB", [128, 1, H], F32)
        kbt = sb(es, "kbt", [128, NT_ALL, 1], F32)
        Fx = sb(es, "Fx", [128, NT_ALL, H], F32)
        TOT = sb(es, "TOT", [128, NT_ALL, H], F32)
        CUM = sb(es, "CUM", [128, NT_ALL, H], F32)
        b_F = Buf()
        b_F2 = Buf()
        S.dma(bfB[:, 0, :], b_f.partition_broadcast(128), w=[b_F])
        S.dma(kbt[:, :, 0], kbneg[:, :], w=[b_F])
        S.op("dve", lambda e: e.tensor_tensor(out=Fx[:], in0=Fraw[:], in1=bfB[:].broadcast_to([128, NT_ALL, H]),
                                              op=ALU.add), r=[b_Fraw, b_F], w=[b_F2])
        S.op("act", lambda e: e.activation(out=Fx[:], in_=Fx[:], func=AF.Exp, scale=-1.0), r=[b_F2], w=[b_F2])
        S.op("act", lambda e: e.activation(out=Fx[:], in_=Fx[:], func=AF.Ln, bias=cst[:, 2:3], scale=1.0),
             r=[b_F2, b_const], w=[b_F2])
        Fx2 = Fx[:].rearrange("p t h -> p (t h)")
        halves = [(0, 264), (264, 520)]
        for (a0, a1) in halves:
            S.op("pe", lambda e, a0=a0, a1=a1: e.matmul(PS[6][:, 0:a1 - a0], lhsT=tri_f[:], rhs=Fx2[:, a0:a1],
                                                        start=True, stop=True), r=[b_F2, b_const], w=[PSB[6]])
            S.op("dve", lambda e, a0=a0, a1=a1: e.tensor_copy(
                out=NBt[:].rearrange("p t h -> p (t h)")[:, a0:a1], in_=PS[6][:, 0:a1 - a0]), r=[PSB[6]], w=[b_NB])
            S.op("pe", lambda e, a0=a0, a1=a1: e.matmul(PS[7][:, 0:a1 - a0], lhsT=ones_f[:], rhs=Fx2[:, a0:a1],
                                                        start=True, stop=True), r=[b_F2, b_const], w=[PSB[7]])
            S.op("dve", lambda e, a0=a0, a1=a1: e.tensor_copy(
                out=TOT[:].rearrange("p t h -> p (t h)")[:, a0:a1], in_=PS[7][:, 0:a1 - a0]), r=[PSB[7]], w=[b_F])
        for h in range(H):
            S.op("dve", lambda e, h=h: e.tensor_tensor_scan(
                out=CUM[:, :, h], data0=ones_f[:, 0:NT_ALL], data1=TOT[:, :, h], initial=0.0,
                op0=ALU.mult, op1=ALU.add), r=[b_F, b_const], w=[b_F])
        S.op("dve", lambda e: e.tensor_tensor(out=CUM[:], in0=CUM[:], in1=TOT[:], op=ALU.subtract), r=[b_F], w=[b_F])
        S.op("dve", lambda e: e.tensor_tensor(out=NBt[:], in0=NBt[:], in1=CUM[:], op=ALU.add), r=[b_F, b_NB], w=[b_NB])
        S.op("pe", lambda e: e.matmul(PS[6][:, 0:NB * H].rearrange("p (i h) -> p i h", h=H), lhsT=sel72_f[:],
                                      rhs=NBt[:, 3:NT_ALL:4, :][:, 0:NB, :], start=True, stop=True),
             r=[b_NB, b_const], w=[PSB[6]])
        S.op("dve", lambda e: e.tensor_copy(out=NREF[:].rearrange("p i h -> p (i h)"), in_=PS[6][:, 0:NB * H]),
             r=[PSB[6]], w=[b_NREF])
        S.op("dve", lambda e: e.tensor_tensor(out=NBt[:], in0=NBt[:],
                                              in1=kbt[:].broadcast_to([128, NT_ALL, H]), op=ALU.add),
             r=[b_F, b_NB, PSB[6]], w=[b_NB])
        if stage == "A":
            dF = dout("dbgF", [128, NT_ALL * H])
            final_evs.append(S.dma(dF[:, :], NBt[:].rearrange("p t h -> p (t h)"), r=[b_NB]))
            dR = dout("dbgR", [128, NB * H])
            final_evs.append(S.dma(dR[:, :], NREF[:].rearrange("p i h -> p (i h)"), r=[b_NREF]))

    es_q = PhaseStack(S)
    QT = sb(es_q, "QT", [128, H, R_OWN], BF16)
    if stage != "C":
        with PhaseStack(S) as es:
            nO = sb(es, "nO", [128, 16, R_OWN], BF16)
            b_nO = [[bufs(2) for _ in range(2)] for _ in range(NB)]
            xts = [sb(es, f"xo{i}", [128, D], F32) for i in range(2)]
            b_xts = bufs(2)
            junk = sb(es, "junk", [128, D], F32)
            ssq = sb(es, "ssq", [128, 1], F32)
            ms = sb(es, "ms", [128, 1], F32)
            rstd = sb(es, "rstd", [128, 1], F32)
            xs = sb(es, "xs", [128, D], BF16)
            nb_ = (junk, ssq, ms, rstd, xs, Buf(), Buf())
            cnt = 0
            for i in range(NB):
                for part, (r0, nr) in enumerate(((0, 128), (128, 8))):
                    sl = cnt % 2
                    cnt += 1
                    S.dma(xts[sl][0:nr, :], hall[own_row(i, r0): own_row(i, r0) + nr, :], w=[b_xts[sl]])
                    c0 = i * BW + r0
                    norm_transpose(nb_, xts[sl], b_xts[sl], nr,
                                   lambda half, c0=c0, nr=nr: nO[:, half * 8:(half + 1) * 8, c0:c0 + nr],
                                   b_nO[i][part], (0, 1))
            all_nO = [b for i in range(NB) for part in range(2) for b in b_nO[i][part]]
            wst = [sb(es, f"wst{i}", [128, 16, 256], F32) for i in range(2)]
            b_wst = bufs(2)
            wbf = [sb(es, f"wbf{i}", [128, 16, 256], BF16) for i in range(2)]
            b_wbf = bufs(2)
            gst = [sb(es, f"gst{i}", [128, R_OWN], BF16) for i in range(2)]
            b_gst = bufs(2)
            ntiles = [(0, 512), (512, 1024), (1024, 1536), (1536, 2048), (2048, 2176)]
            chunks = [("q", OFF_Q + 256 * c, c) for c in range(4)] + \
                     [("ga", OFF_GA + 256 * c, c) for c in range(8)] + [("gb", OFF_GB + 256 * c, c) for c in range(8)]
            gcount = 0
            gsc = 0
            for ci, (kind, off, c) in enumerate(chunks):
                sl = ci % 2
                S.dma(wst[sl][:], w_in[:, off:off + 256].rearrange("(t p) c -> p t c", p=128), w=[b_wst[sl]])
                for dt_ in range(16):
                    cast_scaled("act" if dt_ % 2 else "dve", wbf[sl][:, dt_, :], wst[sl][:, dt_, :], G1[:, dt_:dt_ + 1],
                                [b_wst[sl], b_const], [b_wbf[sl]])
                for sub in range(2):
                    ct = c * 2 + sub
                    if kind != "q":
                        gs = gsc % 2
                        gsc += 1
                    for (n0, n1) in ntiles:
                        pi = 2 + (gcount % 4)
                        gcount += 1
                        for dt_ in range(16):
                            S.op("pe", lambda e, dt_=dt_, sl=sl, sub=sub, pi=pi, n0=n0, n1=n1: e.matmul(
                                PS[pi][:, 0:n1 - n0], lhsT=wbf[sl][:, dt_, sub * 128:(sub + 1) * 128],
                                rhs=nO[:, dt_, n0:n1], start=(dt_ == 0), stop=(dt_ == 15)),
                                r=all_nO + [b_wbf[sl]], w=[PSB[pi]])
                        if kind == "q":
                            S.op("dve", lambda e, ct=ct, pi=pi, n0=n0, n1=n1: e.tensor_copy(
                                out=QT[:, ct, n0:n1], in_=PS[pi][:, 0:n1 - n0]), r=[PSB[pi]], w=[b_QT[ct]])
                        else:
                            S.op("act", lambda e, gs=gs, pi=pi, n0=n0, n1=n1: e.activation(
                                out=gst[gs][:, n0:n1], in_=PS[pi][:, 0:n1 - n0], func=AF.Sigmoid),
                                r=[PSB[pi]], w=[b_gst[gs]])
                    if kind != "q":
                        dst = SGA if kind == "ga" else SGB
                        S.dma(dst[ct, :, :], gst[gs][:], r=[b_gst[gs]], w=[(b_SGAd if kind == "ga" else b_SGBd)[ct]], eng="pool")
            if stage == "A":
                dQ = dout("dbgQ", [128, H * R_OWN], BF16)
                final_evs.append(S.dma(dQ[:, :], QT[:].rearrange("p h r -> p (h r)"), r=b_QT))

    if stage == "A":
        dK = dout("dbgK", [128, R_ALL], BF16)
        final_evs.append(S.dma(dK[:, :], KT[3, :, :], r=b_KTd[3]))
        dV = dout("dbgV", [128, 1024], BF16)
        final_evs.append(S.dma(dV[:, :], VV[7, :, :], r=[b_VVd[7]]))
        dU = dout("dbgU", [128, R_ALL], BF16)
        final_evs.append(S.dma(dU[:, :], UT[5, :, :], r=b_UTd[5]))
        dG = dout("dbgG", [128, R_OWN], BF16)
        final_evs.append(S.dma(dG[:, :], SGB[9, :, :], r=[b_SGBd[9]]))

    if stage in ("B", "full"):
        with PhaseStack(S) as es:
            KTs = [sb(es, f"KTs{i}", [128, R_ALL], BF16) for i in range(2)]
            Vs = [sb(es, f"Vs{i}", [128, NT_ALL, 128], BF16) for i in range(2)]
            b_KV = bufs(2)
            bias = [sb(es, f"bias{i}", [128, NT_ALL], F32) for i in range(2)]
            b_bias = bufs(2)
            PT = [sb(es, f"PT{i}", [128, BW], BF16) for i in range(4)]
            b_PT = bufs(4)
            rec = sb(es, "rec", [128, BW], F32)
            b_rec = Buf()
            ast = [sb(es, f"ast{i}", [128, R_OWN], BF16) for i in range(2)]
            b_ast = bufs(2)
            scale = 128.0 ** -0.5
            bcount = 0
            pcount = 0
            scount = 0
            for h in range(H):
                hb = h % 2
                S.dma(KTs[hb][:], KT[h, :, :], r=b_KTd[h], w=[b_KV[hb]])
                S.dma(Vs[hb][:], VV[:, :, h * 128:(h + 1) * 128].rearrange("t p c -> p t c"), r=b_VVd, w=[b_KV[hb]])
                for i in range(NB):
                    nk = 4 * i + 5
                    bb = bcount % 2
                    bcount += 1
                    S.op("dve", lambda e, bb=bb, nk=nk, i=i, h=h: e.tensor_scalar(
                        out=bias[bb][:, 0:nk], in0=NBt[:, 0:nk, h], scalar1=NREF[:, i, h:h + 1], scalar2=None,
                        op0=ALU.subtract), r=[b_NB, b_NREF], w=[b_bias[bb]])
                    q_ap = QT[:, h, i * BW:(i + 1) * BW]
                    PO, PL = 6, 7

                    def qk(kt):
                        nonlocal scount
                        pi = scount % 4
                        scount += 1
                        masked = kt >= nk - 2
                        S.op("pe", lambda e: e.matmul(PS[pi][:, 0:BW], lhsT=KTs[hb][:, kt * 128:(kt + 1) * 128],
                                                      rhs=q_ap, start=True, stop=not masked),
                             r=[b_KV[hb], b_QT[h]], w=[PSB[pi]])
                        if masked:
                            mi = kt - (nk - 2)
                            S.op("pe", lambda e: e.matmul(PS[pi][:, 0:BW], lhsT=ident_b[:],
                                                          rhs=mask_b[:, mi * BW:(mi + 1) * BW], start=False, stop=True),
                                 r=[b_const], w=[PSB[pi]])
                        return pi

                    pq = []
                    nxt_kt = 0
                    for kt in range(nk):
                        while nxt_kt < nk and len(pq) < 3:
                            pq.append(qk(nxt_kt))
                            nxt_kt += 1
                        pi = pq.pop(0)
                        ps_ = pcount % 4
                        pcount += 1
                        S.op("act", lambda e, pi=pi, ps_=ps_, kt=kt: e.activation(
                            out=PT[ps_][:], in_=PS[pi][:, 0:BW], func=AF.Exp, bias=bias[bb][:, kt:kt + 1], scale=scale),
                            r=[PSB[pi], b_bias[bb]], w=[b_PT[ps_]])
                        S.op("pe", lambda e, ps_=ps_, kt=kt: e.matmul(
                            PS[PO][:, 0:BW], lhsT=Vs[hb][:, kt, :], rhs=PT[ps_][:], start=(kt == 0), stop=(kt == nk - 1)),
                            r=[b_KV[hb], b_PT[ps_]], w=[PSB[PO]])
                        S.op("pe", lambda e, ps_=ps_, kt=kt: e.matmul(
                            PS[PL][:, 0:BW], lhsT=ones_b[:], rhs=PT[ps_][:], start=(kt == 0), stop=(kt == nk - 1)),
                            r=[b_const, b_PT[ps_]], w=[PSB[PL]])
                    S.op("dve", lambda e: e.reciprocal(out=rec[:], in_=PS[PL][:, 0:BW]), r=[PSB[PL]], w=[b_rec])
                    S.op("dve", lambda e, i=i: e.tensor_tensor(out=ast[hb][:, i * BW:(i + 1) * BW], in0=PS[PO][:, 0:BW],
                                                               in1=rec[:], op=ALU.mult),
                         r=[PSB[PO], b_rec], w=[b_ast[hb]])
                S.dma(AT[h, :, :], ast[hb][:], r=[b_ast[hb]], w=[b_ATd[h]], eng="pool")
        if stage == "B":
            dA = dout("dbgA", [H, 128, R_OWN], BF16)
            final_evs.append(S.dma(dA[:, :, :], AT[:, :, :], r=b_ATd))


    es_q.close()

    def xap(tile_ap, off, dims):
        return bass.AP(tile_ap.tensor, tile_ap.offset + off, [list(tile_ap.ap[0])] + [list(d) for d in dims])

    def tt(eng, out, in0, in1, op, r, w):
        return S.op(eng, lambda e: e.tensor_tensor(out=out, in0=in0, in1=in1, op=op), r=r, w=w)

    def ts(eng, out, in0, s1, s2, op0, op1, r, w):
        if op1 is None:
            return S.op(eng, lambda e: e.tensor_scalar(out=out, in0=in0, scalar1=s1, scalar2=None, op0=op0), r=r, w=w)
        return S.op(eng, lambda e: e.tensor_scalar(out=out, in0=in0, scalar1=s1, scalar2=s2, op0=op0, op1=op1), r=r, w=w)

    def stt(out, in0, sc, in1, op0, op1, r, w):
        return S.op("dve", lambda e: e.scalar_tensor_tensor(out=out, in0=in0, scalar=sc, in1=in1, op0=op0, op1=op1), r=r, w=w)

    def cp(eng, out, in_, r, w):
        if eng == "act":
            return S.op("act", lambda e: e.copy(out=out, in_=in_), r=r, w=w)
        return S.op(eng, lambda e: e.tensor_copy(out=out, in_=in_), r=r, w=w)

    def actf(out, in_, func, r, w, **kw):
        return S.op("act", lambda e: e.activation(out=out, in_=in_, func=func, **kw), r=r, w=w)

    if stage in ("C", "full"):
        KS = list(range(9)) + [8 * m for m in range(2, 17)]
        KI = {k: idx for idx, k in enumerate(KS)}
        NK = len(KS)
        PI_ = float(np.pi)
        WSM = dscr("WSM", [8, 128, 8, 4, 2, 128], BF16)
        CYM = dscr("CYM", [8, 128, 8, 4, 2, 128], BF16)
        KBM = dscr("KBM", [8, 128, 8, 128], BF16)
        YG = dscr("YG", [8, 128, R_OWN], BF16)
        b_WSMd = bufs(8)
        b_CYMd = bufs(8)
        b_KBMd = bufs(8)
        b_YGd = bufs(8)
        es_ssm = PhaseStack(S)
        AR = sb(es_ssm, "AR", [128, 16, 32, 1], F32)
        AI = sb(es_ssm, "AI", [128, 16, 32, 1], F32)
        b_A = Buf()
        with PhaseStack(S) as es:
            lamr = sb(es, "lamr", [128, 64], F32)
            lami = sb(es, "lami", [128, 64], F32)
            ldt = sb(es, "ldt", [128, 64], F32)
            bre = sb(es, "bre", [128, 64, 16], F32)
            bim = sb(es, "bim", [128, 64, 16], F32)
            Cre = sb(es, "Cre", [128, 64, 16], F32)
            Cim = sb(es, "Cim", [128, 64, 16], F32)
            cn = [sb(es, f"cn{i}", [128, 128], F32) for i in range(2)]
            b_cn = bufs(2)
            kk = sb(es, "kk", [128, NK, 1], F32)
            rm8 = sb(es, "rm8", [128, 8], F32)
            cmask = sb(es, "cmask", [128, 4, 128], F32)
            bdm = sb(es, "bdm", [128, 128], F32)
            dsk = sb(es, "dsk", [128, 8], F32)
            bP = Buf()
            bC = Buf()
            for hlf in range(2):
                ps_ = slice(64 * hlf, 64 * hlf + 64)
                S.dma(lamr[ps_, :], lam_re.rearrange("g p -> p g"), w=[bP])
                S.dma(lami[ps_, :], lam_im.rearrange("g p -> p g"), w=[bP])
                S.dma(bre[ps_, :, :], b_re.rearrange("g p c -> p g c"), w=[bP])
                S.dma(bim[ps_, :, :], b_im.rearrange("g p c -> p g c"), w=[bP])
            S.dma(ldt[:], log_dt.partition_broadcast(128), w=[bP])
            S.dma(kk[:, :, 0], c_kk.partition_broadcast(128), w=[bP])
            S.dma(rm8[:], c_rowmask8[:, :], w=[bP])
            S.dma(cmask[:].rearrange("p q c -> p (q c)"), c_cmask[:, :], w=[bP])
            S.dma(bdm[:], c_bdmask[:, :], w=[bP])
            S.dma(dsk[:], d_skip.rearrange("(t p) -> p t", p=128), w=[bP])
            k = 0
            for (src, dstt) in ((c_re, Cre), (c_im, Cim)):
                dflat = dstt[:].rearrange("p g c -> p (g c)")
                for t8 in range(8):
                    sl = k % 2
                    k += 1
                    nat = src[8 * t8: 8 * t8 + 8, :, :].rearrange("g c p -> (g c) p")
                    S.dma(cn[sl][:, 0:64], nat, w=[b_cn[sl]])
                    S.dma(cn[sl][:, 64:128], nat, w=[b_cn[sl]])
                    pi = 6 + (k % 2)
                    S.op("pe", lambda e: e.transpose(out=PS[pi][:, 0:128], in_=cn[sl][:, :], identity=ident_f[:]),
                         r=[b_cn[sl], b_const], w=[PSB[pi]])
                    cp("dve", dflat[:, t8 * 128:(t8 + 1) * 128], PS[pi][:, 0:128], [PSB[pi]], [bC])
            dtt = sb(es, "dtt", [128, 64], F32)
            lrdt = sb(es, "lrdt", [128, 1, 64], F32)
            lidt = sb(es, "lidt", [128, 1, 64], F32)
            actf(dtt[:], ldt[:], AF.Exp, [bP], [bP])
            tt("dve", lrdt[:, 0, :], lamr[:], dtt[:], ALU.mult, [bP], [bP])
            tt("dve", lidt[:, 0, :], lami[:], dtt[:], ALU.mult, [bP], [bP])
            ANG = sb(es, "ANG", [128, NK, 64], F32)
            MAG = sb(es, "MAG", [128, NK, 64], F32)
            PRt = sb(es, "PRt", [128, NK, 64, 1], F32)
            PIt = sb(es, "PIt", [128, NK, 64, 1], F32)
            T1 = sb(es, "T1", [128, NK * 64], F32)
            T2 = sb(es, "T2", [128, NK * 64], F32)
            T3 = sb(es, "T3", [128, NK * 64], F32)
            TI = sb(es, "TI", [128, NK * 64], mybir.dt.int32)
            kkb = kk[:].broadcast_to([128, NK, 64])
            tt("dve", ANG[:], kkb, lidt[:].broadcast_to([128, NK, 64]), ALU.mult, [bP], [bP])
            tt("dve", MAG[:], kkb, lrdt[:].broadcast_to([128, NK, 64]), ALU.mult, [bP], [bP])
            actf(MAG[:], MAG[:], AF.Exp, [bP], [bP])
            angf = ANG[:].rearrange("p k g -> p (k g)")
            C1 = 6.28125
            C2 = 2.0 * np.pi - 6.28125

            def range_reduce(shift, dst):
                ts("dve", T1[:], angf, shift, 1.0 / (2 * np.pi), ALU.add, ALU.mult, [bP], [bP])
                cp("dve", TI[:], T1[:], [bP], [bP])
                cp("dve", T2[:], TI[:], [bP], [bP])
                ts("dve", T1[:], angf, shift, None, ALU.add, None, [bP], [bP])
                stt(T3[:], T2[:], -C1, T1[:], ALU.mult, ALU.add, [bP], [bP])
                stt(T1[:], T2[:], -float(C2), T3[:], ALU.mult, ALU.add, [bP], [bP])
                ts("dve", T2[:], T1[:], PI_, -2 * PI_, ALU.is_gt, ALU.mult, [bP], [bP])
                tt("dve", T1[:], T1[:], T2[:], ALU.add, [bP], [bP])
                ts("dve", T2[:], T1[:], -PI_, 2 * PI_, ALU.is_lt, ALU.mult, [bP], [bP])
                tt("dve", dst, T1[:], T2[:], ALU.add, [bP], [bP])

            SINt = sb(es, "SINt", [128, NK * 64], F32)
            COSt = sb(es, "COSt", [128, NK * 64], F32)
            range_reduce(0.0, SINt[:])
            range_reduce(PI_ / 2, COSt[:])
            actf(SINt[:], SINt[:], AF.Sin, [bP], [bP])
            actf(COSt[:], COSt[:], AF.Sin, [bP], [bP])
            magf = MAG[:].rearrange("p k g -> p (k g)")
            tt("dve", PRt[:].rearrange("p k g o -> p (k g o)"), magf, COSt[:], ALU.mult, [bP], [bP])
            tt("dve", PIt[:].rearrange("p k g o -> p (k g o)"), magf, SINt[:], ALU.mult, [bP], [bP])
            for hlf in range(2):
                ps_ = slice(64 * hlf, 64 * hlf + 64)
                cp("dve", AR[ps_, :, :, :], PRt[ps_, 8:24, hlf:64:2, :], [bP], [b_A])
                cp("dve", AI[ps_, :, :, :], PIt[ps_, 8:24, hlf:64:2, :], [bP], [b_A])
            nr = sb(es, "nr", [128, 64], F32)
            den = sb(es, "den", [128, 64], F32)
            tq = sb(es, "tq", [128, 64], F32)
            zr = sb(es, "zr", [128, 64, 1], F32)
            zi = sb(es, "zi", [128, 64, 1], F32)
            p1r = PRt[:, 1, :, 0]
            p1i = PIt[:, 1, :, 0]
            ts("dve", nr[:], p1r, -1.0, None, ALU.add, None, [bP], [bP])
            tt("dve", den[:], lamr[:], lamr[:], ALU.mult, [bP], [bP])
            tt("dve", tq[:], lami[:], lami[:], ALU.mult, [bP], [bP])
            tt("dve", den[:], den[:], tq[:], ALU.add, [bP], [bP])
            S.op("dve", lambda e: e.reciprocal(out=den[:], in_=den[:]), r=[bP], w=[bP])
            tt("dve", zr[:, :, 0], nr[:], lamr[:], ALU.mult, [bP], [bP])
            tt("dve", tq[:], p1i, lami[:], ALU.mult, [bP], [bP])
            tt("dve", zr[:, :, 0], zr[:, :, 0], tq[:], ALU.add, [bP], [bP])
            tt("dve", zr[:, :, 0], zr[:, :, 0], den[:], ALU.mult, [bP], [bP])
            tt("dve", zi[:, :, 0], p1i, lamr[:], ALU.mult, [bP], [bP])
            tt("dve", tq[:], nr[:], lami[:], ALU.mult, [bP], [bP])
            tt("dve", zi[:, :, 0], zi[:, :, 0], tq[:], ALU.subtract, [bP], [bP])
            tt("dve", zi[:, :, 0], zi[:, :, 0], den[:], ALU.mult, [bP], [bP])
            Bbr = sb(es, "Bbr", [128, 64, 16], F32)
            Bbi = sb(es, "Bbi", [128, 64, 16], F32)
            W1 = sb(es, "W1", [128, 64, 16], F32)
            W2 = sb(es, "W2", [128, 64, 16], F32)
            zrb = zr[:].broadcast_to([128, 64, 16])
            zib = zi[:].broadcast_to([128, 64, 16])
            tt("dve", Bbr[:], bre[:], zrb, ALU.mult, [bP], [bP])
            tt("dve", W1[:], bim[:], zib, ALU.mult, [bP], [bP])
            tt("dve", Bbr[:], Bbr[:], W1[:], ALU.subtract, [bP], [bP])
            tt("dve", Bbi[:], bim[:], zrb, ALU.mult, [bP], [bP])
            tt("dve", W1[:], bre[:], zib, ALU.mult, [bP], [bP])
            tt("dve", Bbi[:], Bbi[:], W1[:], ALU.add, [bP], [bP])
            BS = sb(es, "BS", [128, 1024], F32)
            cp("dve", BS[0:64, :], Bbr[0:64].rearrange("p g c -> p (g c)"), [bP], [bP])
            cp("dve", BS[64:128, :], Bbi[64:128].rearrange("p g c -> p (g c)"), [bP], [bP])
            Yre = sb(es, "Yre", [128, 64, 16], F32)
            Yim = sb(es, "Yim", [128, 64, 16], F32)
            CAS = sb(es, "CAS", [128, 1024], F32)
            KBt = sb(es, "KBt", [128, 8, 8, 128], BF16)
            b_KBt = Buf()
            stgc = [sb(es, f"stgc{i}", [128, 4, 2, 128], BF16) for i in range(2)]
            b_stgc = bufs(2)
            ktmp = sb(es, "ktmp", [128, 128], F32)
            sc = 0
            for k in range(9):
                prb = PRt[:, k, :, :].broadcast_to([128, 64, 16])
                pib = PIt[:, k, :, :].broadcast_to([128, 64, 16])
                tt("dve", W1[:], Cre[:], prb, ALU.mult, [bP, bC], [bP])
                tt("dve", W2[:], Cim[:], pib, ALU.mult, [bP, bC], [bP])
                tt("dve", Yre[:], W1[:], W2[:], ALU.subtract, [bP], [bP])
                tt("dve", W1[:], Cre[:], pib, ALU.mult, [bP, bC], [bP])
                tt("dve", W2[:], Cim[:], prb, ALU.mult, [bP, bC], [bP])
                stt(Yim[:].rearrange("p g c -> p (g c)"), W1[:].rearrange("p g c -> p (g c)"), -1.0,
                    W2[:].rearrange("p g c -> p (g c)"), ALU.mult, ALU.subtract, [bP], [bP])
                yref = Yre[:].rearrange("p g c -> p (g c)")
                yimf = Yim[:].rearrange("p g c -> p (g c)")
                if k <= 7:
                    cp("dve", CAS[0:64, :], yref[0:64, :], [bP], [bP])
                    cp("dve", CAS[64:128, :], yimf[64:128, :], [bP], [bP])
                    for ft in range(8):
                        pi = 4 + (ft % 2)
                        S.op("pe", lambda e: e.matmul(PS[pi][:, 0:128], lhsT=BS[:, ft * 128:(ft + 1) * 128],
                                                      rhs=CAS[:, ft * 128:(ft + 1) * 128], start=True, stop=True),
                             r=[bP], w=[PSB[pi]])
                        if k == 0:
                            tt("dve", ktmp[:], PS[pi][:, 0:128], bdm[:], ALU.mult, [PSB[pi], bP], [bP])
                            stt(KBt[:, ft, k, :], ident_f[:], dsk[:, ft:ft + 1], ktmp[:], ALU.mult, ALU.add,
                                [bP, b_const], [b_KBt])
                        else:
                            tt("dve", KBt[:, ft, k, :], PS[pi][:, 0:128], bdm[:], ALU.mult, [PSB[pi], bP], [b_KBt])
                if k >= 1:
                    tau = k - 1
                    for ft in range(8):
                        sl = sc % 2
                        sc += 1
                        for ri, yf in enumerate((yref, yimf)):
                            src = yf[:, ft * 128:(ft + 1) * 128]
                            srcb = xap(src, 0, [[0, 4], [1, 128]])
                            tt("dve", stgc[sl][:, :, ri, :], srcb, cmask[:], ALU.mult, [bP], [b_stgc[sl]])
                        S.dma(CYM[ft, :, tau, :, :, :], stgc[sl][:], r=[b_stgc[sl]], w=[b_CYMd[ft]], eng="pool")
            for ft in range(8):
                S.dma(KBM[ft, :, :, :], KBt[:, ft, :, :], r=[b_KBt], w=[b_KBMd[ft]], eng="pool")
            XR = Yre
            XI = Yim
            for s_ in range(8):
                k = 7 - s_
                prb = PRt[:, k, :, :].broadcast_to([128, 64, 16])
                pib = PIt[:, k, :, :].broadcast_to([128, 64, 16])
                tt("dve", W1[:], Bbr[:], prb, ALU.mult, [bP], [bP])
                tt("dve", W2[:], Bbi[:], pib, ALU.mult, [bP], [bP])
                tt("dve", XR[:], W1[:], W2[:], ALU.subtract, [bP], [bP])
                tt("dve", W1[:], Bbr[:], pib, ALU.mult, [bP], [bP])
                tt("dve", W2[:], Bbi[:], prb, ALU.mult, [bP], [bP])
                tt("dve", XI[:], W1[:], W2[:], ALU.add, [bP], [bP])
                for ft in range(8):
                    sl = sc % 2
                    sc += 1
                    for ri, xf in enumerate((XR, XI)):
                        pi = 6 + ri
                        xin = xf[0:64].rearrange("p g c -> p (g c)")[:, ft * 128:(ft + 1) * 128]
                        S.op("pe", lambda e: e.transpose(out=PS[pi][:, 0:64], in_=xin, identity=ident_f[0:64, 0:64]),
                             r=[bP, b_const], w=[PSB[pi]])
                        srcb = xap(PS[pi][:, 0:64], 0, [[0, 4], [0, 2], [1, 64]])
                        mskb = xap(rm8[:], 0, [[2, 4], [1, 2], [0, 64]])
                        outv = stgc[sl][:, :, ri, :].rearrange("p q (h c) -> p q h c", h=2)
                        tt("dve", outv, srcb, mskb, ALU.mult, [PSB[pi], bP], [b_stgc[sl]])
                    S.dma(WSM[ft, :, s_, :, :, :], stgc[sl][:], r=[b_stgc[sl]], w=[b_WSMd[ft]], eng="pool")

        with PhaseStack(S) as es:
            u2 = [sb(es, f"u2{i}", [128, R_ALL], BF16) for i in range(2)]
            b_u2 = bufs(2)
            wsb = [sb(es, f"wsb{i}", [128, 8, 4, 2, 128], BF16) for i in range(2)]
            b_wsb = bufs(2)
            kbb = [sb(es, f"kbb{i}", [128, 8, 128], BF16) for i in range(2)]
            b_kbb = bufs(2)
            SR = sb(es, "SR", [128, 8, 1040], F32)
            SI = sb(es, "SI", [128, 8, 1040], F32)
            b_HR = bufs(16)
            b_HI = bufs(16)
            ER = [sb(es, f"ER{i}", [128, 8, 65], F32) for i in range(2)]
            EI = [sb(es, f"EI{i}", [128, 8, 65], F32) for i in range(2)]
            b_ER = bufs(2)
            b_EI = bufs(2)
            AdR = [sb(es, f"AdR{i}", [128, 8, 1], F32) for i in range(2)]
            AdI = [sb(es, f"AdI{i}", [128, 8, 1], F32) for i in range(2)]
            b_Ad = bufs(2)
            sq = [sb(es, f"sq{i}", [128, 8, 1], F32) for i in range(3)]
            b_sq = Buf()
            t1 = sb(es, "t1", [128, 8, 65], F32)
            t2 = sb(es, "t2", [128, 8, 65], F32)
            t3 = sb(es, "t3", [128, 8, 65], F32)
            t4 = sb(es, "t4", [128, 8, 65], F32)
            b_t12 = Buf()
            b_t34 = Buf()
            HbR = sb(es, "HbR", [128, 8, NB, 17], BF16)
            HbI = sb(es, "HbI", [128, 8, NB, 17], BF16)
            b_Hb = bufs(2)
            ygst = [sb(es, f"ygst{i}", [128, NB, BW], BF16) for i in range(2)]
            b_ygst = bufs(2)
            g1 = sb(es, "g1", [128, NB * 17], F32)
            g2 = sb(es, "g2", [128, NB * 17], F32)
            b_g = Buf()
            NM = 260
            ecount = 0
            for gbt in range(4):
                Q0 = 8 * gbt
                for ftl in range(2):
                    ft = 2 * gbt + ftl
                    S.dma(u2[ftl][:], UT[ft, :, :], r=b_UTd[ft], w=[b_u2[ftl]])
                    S.dma(wsb[ftl][:], WSM[ft, :, :, :, :, :], r=[b_WSMd[ft]], w=[b_wsb[ftl]])
                    S.dma(kbb[ftl][:], KBM[ft, :, :, :], r=[b_KBMd[ft]], w=[b_kbb[ftl]])
                for ftl in range(2):
                    for qq in range(4):
                        for ri in range(2):
                            dstt = SR if ri == 0 else SI
                            bd = b_HR if ri == 0 else b_HI
                            for mc in range(4):
                                pi = ecount % 4
                                ecount += 1
                                for s_ in range(8):
                                    st = NM * mc * 8 + s_
                                    S.op("pe", lambda e: e.matmul(
                                        PS[pi][:, 0:NM], lhsT=wsb[ftl][:, s_, qq, ri, :],
                                        rhs=u2[ftl][:, st: st + 8 * (NM - 1) + 1: 8], start=(s_ == 0), stop=(s_ == 7)),
                                        r=[b_u2[ftl], b_wsb[ftl]], w=[PSB[pi]])
                                cp("act" if ecount % 2 else "dve", dstt[:, 4 * ftl + qq, NM * mc: NM * (mc + 1)],
                                   PS[pi][:, 0:NM], [PSB[pi]], bd)
                HRv = SR[:].rearrange("p q (M u) -> p q M u", u=16)
                HIv = SI[:].rearrange("p q (M u) -> p q M u", u=16)

                def cmul_acc(curR, curI, aR, aI, pR, pI, rR, rI, wR, wI):
                    n = curR.shape[2]
                    tt("dve", t1[:, :, 0:n], aR, pR, ALU.mult, rR + [b_A], [b_t12])
                    tt("dve", t2[:, :, 0:n], aI, pI, ALU.mult, rI + [b_A], [b_t12])
                    tt("dve", curR, curR, t1[:, :, 0:n], ALU.add, [b_t12], wR)
                    tt("dve", curR, curR, t2[:, :, 0:n], ALU.subtract, [b_t12], wR)
                    tt("dve", t3[:, :, 0:n], aR, pI, ALU.mult, rI + [b_A], [b_t34])
                    tt("dve", t4[:, :, 0:n], aI, pR, ALU.mult, rR + [b_A], [b_t34])
                    tt("dve", curI, curI, t3[:, :, 0:n], ALU.add, [b_t34], wI)
                    tt("dve", curI, curI, t4[:, :, 0:n], ALU.add, [b_t34], wI)

                a1R = AR[:, 0, Q0:Q0 + 8, :].broadcast_to([128, 8, 65])
                a1I = AI[:, 0, Q0:Q0 + 8, :].broadcast_to([128, 8, 65])
                for mu in range(1, 16):
                    cmul_acc(HRv[:, :, :, mu], HIv[:, :, :, mu], a1R, a1I, HRv[:, :, :, mu - 1], HIv[:, :, :, mu - 1],
                             [b_HR[mu - 1]], [b_HI[mu - 1]], [b_HR[mu]], [b_HI[mu]])
                cp("dve", ER[0][:], HRv[:, :, :, 15], [b_HR[15]], [b_ER[0]])
                cp("dve", EI[0][:], HIv[:, :, :, 15], [b_HI[15]], [b_EI[0]])
                cp("dve", AdR[0][:], AR[:, 15, Q0:Q0 + 8, :], [b_A], [b_Ad[0]])
                cp("dve", AdI[0][:], AI[:, 15, Q0:Q0 + 8, :], [b_A], [b_Ad[0]])
                cur = 0
                d = 1
                while d < 65:
                    nxt = 1 - cur
                    cp("dve", ER[nxt][:], ER[cur][:], [b_ER[cur]], [b_ER[nxt]])
                    cp("dve", EI[nxt][:], EI[cur][:], [b_EI[cur]], [b_EI[nxt]])
                    n = 65 - d
                    adR = AdR[cur][:].broadcast_to([128, 8, n])
                    adI = AdI[cur][:].broadcast_to([128, 8, n])
                    tt("dve", t1[:, :, 0:n], adR, ER[cur][:, :, 0:n], ALU.mult, [b_Ad[cur], b_ER[cur]], [b_t12])
                    tt("dve", t2[:, :, 0:n], adI, EI[cur][:, :, 0:n], ALU.mult, [b_Ad[cur], b_EI[cur]], [b_t12])
                    tt("dve", ER[nxt][:, :, d:65], ER[nxt][:, :, d:65], t1[:, :, 0:n], ALU.add, [b_t12], [b_ER[nxt]])
                    tt("dve", ER[nxt][:, :, d:65], ER[nxt][:, :, d:65], t2[:, :, 0:n], ALU.subtract, [b_t12], [b_ER[nxt]])
                    tt("dve", t3[:, :, 0:n], adR, EI[cur][:, :, 0:n], ALU.mult, [b_Ad[cur], b_EI[cur]], [b_t34])
                    tt("dve", t4[:, :, 0:n], adI, ER[cur][:, :, 0:n], ALU.mult, [b_Ad[cur], b_ER[cur]], [b_t34])
                    tt("dve", EI[nxt][:, :, d:65], EI[nxt][:, :, d:65], t3[:, :, 0:n], ALU.add, [b_t34], [b_EI[nxt]])
                    tt("dve", EI[nxt][:, :, d:65], EI[nxt][:, :, d:65], t4[:, :, 0:n], ALU.add, [b_t34], [b_EI[nxt]])
                    tt("dve", sq[0][:], AdR[cur][:], AdR[cur][:], ALU.mult, [b_Ad[cur]], [b_sq])
                    tt("dve", sq[1][:], AdI[cur][:], AdI[cur][:], ALU.mult, [b_Ad[cur]], [b_sq])
                    tt("dve", sq[2][:], AdR[cur][:], AdI[cur][:], ALU.mult, [b_Ad[cur]], [b_sq])
                    tt("dve", AdR[nxt][:], sq[0][:], sq[1][:], ALU.subtract, [b_sq], [b_Ad[nxt]])
                    ts("dve", AdI[nxt][:], sq[2][:], 2.0, None, ALU.mult, None, [b_sq], [b_Ad[nxt]])
                    cur = nxt
                    d *= 2
                for mu in range(15):
                    aR = AR[:, mu, Q0:Q0 + 8, :].broadcast_to([128, 8, 64])
                    aI = AI[:, mu, Q0:Q0 + 8, :].broadcast_to([128, 8, 64])
                    cmul_acc(HRv[:, :, 1:65, mu], HIv[:, :, 1:65, mu], aR, aI, ER[cur][:, :, 0:64], EI[cur][:, :, 0:64],
                             [b_ER[cur]], [b_EI[cur]], [b_HR[mu]], [b_HI[mu]])
                cp("dve", HRv[:, :, :, 15], ER[cur][:], [b_ER[cur]], [b_HR[15]])
                cp("dve", HIv[:, :, :, 15], EI[cur][:], [b_EI[cur]], [b_HI[15]])
                cp("dve", HbR[:], xap(SR[:], 48, [[1040, 8], [64, NB], [1, 17]]), b_HR, [b_Hb[0]])
                cp("dve", HbI[:], xap(SI[:], 48, [[1040, 8], [64, NB], [1, 17]]), b_HI, [b_Hb[1]])
                for ftl in range(2):
                    ft = 2 * gbt + ftl
                    S.dma(wsb[ftl][:], CYM[ft, :, :, :, :, :], r=[b_CYMd[ft]], w=[b_wsb[ftl]])
                for ftl in range(2):
                    ft = 2 * gbt + ftl
                    for tau in range(8):
                        pi = 4 + (tau % 2)
                        po = PS[pi][:, 0:NB * 17]
                        first = True
                        for s_ in range(tau + 1):
                            S.op("pe", lambda e: e.matmul(po, lhsT=kbb[ftl][:, tau - s_, :],
                                                          rhs=xap(u2[ftl][:], 392 + s_, [[512, NB], [8, 17]]),
                                                          start=first, stop=False),
                                 r=[b_u2[ftl], b_kbb[ftl]], w=[PSB[pi]])
                            first = False
                        for qq in range(4):
                            for ri in range(2):
                                hb_ = HbR if ri == 0 else HbI
                                last = (qq == 3 and ri == 1)
                                S.op("pe", lambda e: e.matmul(po, lhsT=wsb[ftl][:, tau, qq, ri, :],
                                                              rhs=hb_[:, 4 * ftl + qq, :, :], start=False, stop=last),
                                     r=[b_Hb[ri], b_wsb[ftl]], w=[PSB[pi]])
                        actf(g1[:], po, AF.Square, [PSB[pi]], [b_g])
                        ts("dve", g1[:], g1[:], 0.044715, 1.0, ALU.mult, ALU.add, [b_g], [b_g])
                        tt("dve", g1[:], g1[:], po, ALU.mult, [b_g, PSB[pi]], [b_g])
                        actf(g2[:], g1[:], AF.Sigmoid, [b_g], [b_g], scale=1.5957691216057308)
                        outv = xap(ygst[ftl][:], tau, [[BW, NB], [8, 17]])
                        tt("dve", outv, g2[:].rearrange("p (i m) -> p i m", m=17), po.rearrange("p (i m) -> p i m", m=17),
                           ALU.mult, [b_g, PSB[pi]], [b_ygst[ftl]])
                    S.dma(YG[ft, :, :], ygst[ftl][:].rearrange("p i q -> p (i q)"), r=[b_ygst[ftl]], w=[b_YGd[ft]], eng="pool")
        es_ssm.close()
        if stage == "C":
            dY = dout("dbgY", [8, 128, R_OWN], BF16)
            final_evs.append(S.dma(dY[:, :, :], YG[:, :, :], r=b_YGd))

    def own_segments(r0, r1):
        segs = []
        r = r0
        while r < r1:
            i, q = divmod(r, BW)
            n = min(r1 - r, BW - q)
            segs.append((r, own_row(i, q), n, i, q))
            r += n
        return segs

    if stage == "full":
        NTO = R_OWN // 128
        H1 = dscr("H1", [R_OWN, D], F32)
        ACTS = dscr("ACTS", [NTO, 128, 44, 128], BF16)
        b_H1d = [bufs(4) for _ in range(NTO)]
        b_ACTSd = bufs(44)
        out_t = dout("out", [NB * 128, D])
        ntiles5 = [(0, 512), (512, 1024), (1024, 1536), (1536, 2048), (2048, 2176)]
        es_mn = PhaseStack(S)
        MN = sb(es_mn, "MN", [128, 16, R_OWN], BF16)
        b_MN = [bufs(2) for _ in range(NTO)]
        G2 = sb(es_mn, "G2", [128, 16], F32)
        b_G2 = Buf()
        S.dma(G2[:], g_ffn.rearrange("(t p) -> p t", p=128), w=[b_G2])
        with PhaseStack(S) as es:
            ygT = sb(es, "ygT", [128, 8, R_OWN], BF16)
            atT = sb(es, "atT", [128, 8, R_OWN], BF16)
            b_yg = Buf()
            b_at = Buf()
            for kt in range(8):
                S.dma(ygT[:, kt, :], YG[kt, :, :], r=[b_YGd[kt]], w=[b_yg])
                S.dma(atT[:, kt, :], AT[kt, :, :], r=[b_ATd[kt]], w=[b_at])
            wst3 = [sb(es, f"w3s{i}", [128, 8, 128], F32) for i in range(3)]
            b_wst3 = bufs(3)
            wb3 = [[sb(es, f"w3b{i}_{j}", [128, 8, 128], BF16) for i in range(3)] for j in range(2)]
            b_wb3 = [bufs(3) for _ in range(2)]
            sgt = [sb(es, f"sg{i}", [128, R_OWN], BF16) for i in range(2)]
            b_sgt = bufs(2)
            sbz = sb(es, "sbz", [128, 512], F32)
            m1 = sb(es, "m1", [128, 512], F32)
            m2 = sb(es, "m2", [128, 512], F32)
            b_m = Buf()
            pc = 0
            for ct in range(16):
                wsl = ct % 2
                srcs = (w_glu[:, ct * 128:(ct + 1) * 128], w_glu[:, 2048 + ct * 128: 2048 + (ct + 1) * 128],
                        w_attn_o[:, ct * 128:(ct + 1) * 128])
                for wi in range(3):
                    S.dma(wst3[wi][:], srcs[wi].rearrange("(t p) c -> p t c", p=128), w=[b_wst3[wi]])
                    cast_plain("act", wb3[wsl][wi][:], wst3[wi][:], [b_wst3[wi]], [b_wb3[wsl][wi]])
                S.dma(sgt[0][:], SGA[ct, :, :], r=[b_SGAd[ct]], w=[b_sgt[0]])
                S.dma(sgt[1][:], SGB[ct, :, :], r=[b_SGBd[ct]], w=[b_sgt[1]])
                for (n0, n1) in ntiles5:
                    N = n1 - n0
                    pis = [(pc + x) % 8 for x in range(3)]
                    pc += 3
                    for wi, act_in, bact in ((0, ygT, b_yg), (1, ygT, b_yg), (2, atT, b_at)):
                        pi = pis[wi]
                        for kt in range(8):
                            S.op("pe", lambda e: e.matmul(PS[pi][:, 0:N], lhsT=wb3[wsl][wi][:, kt, :],
                                                          rhs=act_in[:, kt, n0:n1], start=(kt == 0), stop=(kt == 7)),
                                 r=[b_wb3[wsl][wi], bact], w=[PSB[pi]])
                    actf(sbz[:, 0:N], PS[pis[1]][:, 0:N], AF.Sigmoid, [PSB[pis[1]]], [b_m])
                    tt("dve", m1[:, 0:N], PS[pis[0]][:, 0:N], sbz[:, 0:N], ALU.mult, [PSB[pis[0]], b_m], [b_m])
                    tt("dve", m1[:, 0:N], m1[:, 0:N], sgt[0][:, n0:n1], ALU.mult, [b_m, b_sgt[0]], [b_m])
                    tt("dve", m2[:, 0:N], PS[pis[2]][:, 0:N], sgt[1][:, n0:n1], ALU.mult,
                       [PSB[pis[2]], b_sgt[1]], [b_m])
                    wtok = [b_MN[t][ct // 8] for t in range(n0 // 128, (n1 + 127) // 128)]
                    tt("dve", MN[:, ct, n0:n1], m1[:, 0:N], m2[:, 0:N], ALU.add, [b_m], wtok)
        if DEBUG:
            for nm_, src_, toks_ in (("dbgAT", AT, b_ATd), ("dbgYG", YG, b_YGd), ("dbgSGA", SGA, b_SGAd), ("dbgSGB", SGB, b_SGBd)):
                dd_ = dout(nm_, list(src_.shape), BF16)
                final_evs.append(S.dma(dd_[:, :, :], src_[:, :, :], r=toks_))
            dM = dout("dbgM", [128, 16 * R_OWN], BF16)
            final_evs.append(S.dma(dM[:, :], MN[:].rearrange("p c r -> p (c r)"), r=[b for t in range(NTO) for b in b_MN[t]]))
        with PhaseStack(S) as es:
            Wo = sb(es, "Wo", [128, 16, D], BF16)
            b_Wo = bufs(16)
            wos = [sb(es, f"wos{i}", [128, 16, 128], F32) for i in range(2)]
            b_wos = bufs(2)
            for c8 in range(16):
                sl = c8 % 2
                S.dma(wos[sl][:], w_out[:, c8 * 128:(c8 + 1) * 128].rearrange("(t p) c -> p t c", p=128), w=[b_wos[sl]])
                cast_plain("act" if c8 % 2 else "dve", Wo[:, :, c8 * 128:(c8 + 1) * 128], wos[sl][:], [b_wos[sl]], [b_Wo[c8]])
            xo = [sb(es, "xo4", [128, D], F32)] * 2
            b_xo = [Buf()] * 2
            h1t = [sb(es, f"h1t{i}", [128, D], F32) for i in range(2)]
            b_h1t = bufs(2)
            junk = sb(es, "junk4", [128, D], F32)
            ssq = sb(es, "ssq4", [128, 1], F32)
            ms = sb(es, "ms4", [128, 1], F32)
            rstd = sb(es, "rstd4", [128, 1], F32)
            xs = sb(es, "xs4", [128, D], BF16)
            nb_ = (junk, ssq, ms, rstd, xs, Buf(), Buf())
            for t in range(NTO):
                sl = t % 2
                for (r, hr, n, i_, q_) in own_segments(128 * t, 128 * t + 128):
                    S.dma(xo[sl][r - 128 * t: r - 128 * t + n, :], hall[hr:hr + n, :], w=[b_xo[sl]])
                for cc in range(4):
                    for dt_ in range(16):
                        S.op("pe", lambda e: e.matmul(PS[cc][:, :], lhsT=MN[:, dt_, 128 * t:128 * t + 128],
                                                      rhs=Wo[:, dt_, cc * 512:(cc + 1) * 512], start=(dt_ == 0), stop=(dt_ == 15)),
                             r=b_MN[t] + b_Wo[4 * cc:4 * cc + 4], w=[PSB[cc]])
                    tt("dve", h1t[sl][:, cc * 512:(cc + 1) * 512], PS[cc][:, :], xo[sl][:, cc * 512:(cc + 1) * 512], ALU.add,
                       [PSB[cc], b_xo[sl]], [b_h1t[sl]])
                S.dma(H1[128 * t:128 * t + 128, :], h1t[sl][:], r=[b_h1t[sl]], w=b_H1d[t], eng="pool")
                norm_transpose(nb_, h1t[sl], b_h1t[sl], 128,
                               lambda half, t=t: MN[:, half * 8:(half + 1) * 8, 128 * t:128 * t + 128], b_MN[t], (4, 5))
        if DEBUG:
            dH = dout("dbgH1", [R_OWN, D])
            final_evs.append(S.dma(dH[:, :], H1[:, :], r=[b for t in range(NTO) for b in b_H1d[t]]))
            dN = dout("dbgN2", [128, 16 * R_OWN], BF16)
            final_evs.append(S.dma(dN[:, :], MN[:].rearrange("p c r -> p (c r)"), r=[b for t in range(NTO) for b in b_MN[t]]))
        all_MN = [b for t in range(NTO) for b in b_MN[t]]
        with PhaseStack(S) as es:
            CW = sb(es, "CW", [128, 44, 3], F32)
            CB = sb(es, "CB", [128, 44], F32)
            b_cw = Buf()
            for j3 in range(3):
                S.dma(CW[:, :, j3], conv_w[j3, :].rearrange("(t p) -> p t", p=128), w=[b_cw])
            S.dma(CB[:], conv_b.rearrange("(t p) -> p t", p=128), w=[b_cw])
            wus = [sb(es, f"wus{i}", [128, 16, 256], F32) for i in range(2)]
            b_wus = bufs(2)
            wub = [[sb(es, f"wub{i}_{j}", [128, 16, 256], BF16) for i in range(2)] for j in range(2)]
            b_wub = [bufs(2) for _ in range(2)]
            graw = sb(es, "graw", [128, NB, BW], F32)
            tg = sb(es, "tg", [128, NB, BW], F32)
            ubuf = sb(es, "ubuf", [128, NB, BW], BF16)
            b_graw = Buf()
            b_tg = Buf()
            b_ub = Buf()
            acst = [sb(es, f"acst{i}", [128, R_OWN], BF16) for i in range(2)]
            b_acst = bufs(2)
            grf = graw[:].rearrange("p b q -> p (b q)")
            ubf = ubuf[:].rearrange("p b q -> p (b q)")
            pc = 0
            def wload(cpi_):
                for wi, off in enumerate((cpi_ * 256, DFF + cpi_ * 256)):
                    S.dma(wus[wi][:], w_up[:, off:off + 256].rearrange("(t p) c -> p t c", p=128), w=[b_wus[wi]])

            def wcast(cpi_):
                for wi in range(2):
                    for dt_ in range(16):
                        cast_scaled("act" if wi == 0 else "dve", wub[cpi_ % 2][wi][:, dt_, :], wus[wi][:, dt_, :], G2[:, dt_:dt_ + 1],
                                    [b_wus[wi], b_G2], [b_wub[cpi_ % 2][wi]])

            wload(0)
            wcast(0)
            for cpi in range(22):
                wsl = cpi % 2
                if cpi + 1 < 22:
                    wload(cpi + 1)
                for sub in range(2):
                    if sub == 1 and cpi + 1 < 22:
                        wcast(cpi + 1)
                    c = 2 * cpi + sub
                    asl = c % 2
                    for (n0, n1) in ntiles5:
                        N = n1 - n0
                        pg = pc % 8
                        pu = (pc + 1) % 8
                        pc += 2
                        for wi, pi in ((0, pg), (1, pu)):
                            for dt_ in range(16):
                                S.op("pe", lambda e: e.matmul(PS[pi][:, 0:N], lhsT=wub[wsl][wi][:, dt_, sub * 128:(sub + 1) * 128],
                                                              rhs=MN[:, dt_, n0:n1], start=(dt_ == 0), stop=(dt_ == 15)),
                                     r=all_MN + [b_wub[wsl][wi]], w=[PSB[pi]])
                        S.op("act", lambda e: e.copy(out=grf[:, n0:n1], in_=PS[pg][:, 0:N]), r=[PSB[pg]], w=[b_graw])
                        S.op("act", lambda e: e.copy(out=ubf[:, n0:n1], in_=PS[pu][:, 0:N]), r=[PSB[pu]], w=[b_ub])
                    ts("dve", tg[:], graw[:], CW[:, c, 2:3], CB[:, c:c + 1], ALU.mult, ALU.add, [b_graw, b_cw], [b_tg])
                    stt(tg[:, :, 1:BW], graw[:, :, 0:BW - 1], CW[:, c, 1:2], tg[:, :, 1:BW], ALU.mult, ALU.add,
                        [b_graw, b_cw, b_tg], [b_tg])
                    stt(tg[:, :, 2:BW], graw[:, :, 0:BW - 2], CW[:, c, 0:1], tg[:, :, 2:BW], ALU.mult, ALU.add,
                        [b_graw, b_cw, b_tg], [b_tg])
                    actf(tg[:], tg[:], AF.Silu, [b_tg], [b_tg])
                    tt("dve", acst[asl][:].rearrange("p (b q) -> p b q", q=BW), tg[:], ubuf[:], ALU.mult,
                       [b_tg, b_ub], [b_acst[asl]])
                    S.dma(ACTS[:, :, c, :].rearrange("t p r -> p t r"), acst[asl][:].rearrange("p (t r) -> p t r", r=128),
                          r=[b_acst[asl]], w=[b_ACTSd[c]], eng="pool")
        es_mn.close()
        with PhaseStack(S) as es:
            Wd = [sb(es, f"Wd{i}", [128, 44, 512], BF16) for i in range(2)]
            b_Wd = [bufs(4) for _ in range(2)]
            wds = [sb(es, f"wds{i}", [128, 11, 512], F32) for i in range(2)]
            b_wds = bufs(2)
            aT = [sb(es, f"aT{i}", [128, 44, 128], BF16) for i in range(2)]
            b_aT = bufs(2)
            hc = [sb(es, f"hc{i}", [128, 512], F32) for i in range(2)]
            b_hc = bufs(2)
            kctr = [0]

            def wd_load_cast(cc_):
                for piece in range(4):
                    sl_ = kctr[0] % 2
                    kctr[0] += 1
                    S.dma(wds[sl_][:], w_down[piece * 1408:(piece + 1) * 1408, cc_ * 512:(cc_ + 1) * 512].rearrange(
                        "(t p) c -> p t c", p=128), w=[b_wds[sl_]])
                    cast_plain("act", Wd[cc_ % 2][:, piece * 11:(piece + 1) * 11, :], wds[sl_][:], [b_wds[sl_]],
                               [b_Wd[cc_ % 2][piece]])

            wd_load_cast(0)
            for cc in range(4):
                for t in range(NTO):
                    if t == 2 and cc + 1 < 4:
                        wd_load_cast(cc + 1)
                    sl = t % 2
                    S.dma(aT[sl][:], ACTS[t, :, :, :], r=b_ACTSd, w=[b_aT[sl]])
                    S.dma(hc[sl][:], H1[128 * t:128 * t + 128, cc * 512:(cc + 1) * 512], r=[b_H1d[t][cc]], w=[b_hc[sl]])
                    pi = t % 2
                    for c in range(44):
                        S.op("pe", lambda e: e.matmul(PS[pi][:, :], lhsT=aT[sl][:, c, :], rhs=Wd[cc % 2][:, c, :],
                                                      start=(c == 0), stop=(c == 43)),
                             r=[b_aT[sl], b_Wd[cc % 2][c // 11]], w=[PSB[pi]])
                    tt("dve", hc[sl][:], hc[sl][:], PS[pi][:, :], ALU.add, [PSB[pi], b_hc[sl]], [b_hc[sl]])
                    S.dma(H1[128 * t:128 * t + 128, cc * 512:(cc + 1) * 512], hc[sl][:], r=[b_hc[sl]], w=[b_H1d[t][cc]], eng="pool")
        with PhaseStack(S) as es:
            GF = sb(es, "GF", [128, D], F32)
            b_GF = Buf()
            S.dma(GF[:], g_final.partition_broadcast(128), w=[b_GF])
            hf_ = [sb(es, f"hf{i}", [128, D], F32) for i in range(2)]
            b_hf = bufs(2)
            of_ = [sb(es, f"of{i}", [128, D], F32) for i in range(2)]
            b_of = bufs(2)
            junk = sb(es, "junk6", [128, D], F32)
            ssq = sb(es, "ssq6", [128, 1], F32)
            ms = sb(es, "ms6", [128, 1], F32)
            rstd = sb(es, "rstd6", [128, 1], F32)
            b_n = Buf()
            for t in range(NTO):
                sl = t % 2
                S.dma(hf_[sl][:], H1[128 * t:128 * t + 128, :], r=b_H1d[t], w=[b_hf[sl]])
                S.op("act", lambda e: e.activation(out=junk[:], in_=hf_[sl][:], func=AF.Square), r=[b_hf[sl]], w=[b_n])
                w_ = D // 2
                while w_ >= 1:
                    S.op("dve", lambda e, w_=w_: e.tensor_tensor(out=junk[:, 0:w_], in0=junk[:, 0:w_], in1=junk[:, w_:2 * w_],
                                                                 op=ALU.add), r=[b_n], w=[b_n])
                    w_ //= 2
                ssq = junk
                actf(ms[:], ssq[:, 0:1], AF.Sqrt, [b_n, b_const], [b_n], scale=1.0 / D, bias=cst[:, 0:1])
                S.op("dve", lambda e: e.reciprocal(out=rstd[:], in_=ms[:]), r=[b_n], w=[b_n])
                stt(of_[sl][:], hf_[sl][:], rstd[:, 0:1], GF[:], ALU.mult, ALU.mult, [b_n, b_hf[sl], b_GF], [b_of[sl]])
                for (r, hr, n, i_, q_) in own_segments(128 * t, 128 * t + 128):
                    qa = max(q_, 8)
                    if qa >= q_ + n:
                        continue
                    la = r + (qa - q_) - 128 * t
                    cnt_ = q_ + n - qa
                    ev = S.dma(out_t[i_ * 128 + qa - 8: i_ * 128 + qa - 8 + cnt_, :], of_[sl][la:la + cnt_, :], r=[b_of[sl]], eng="pool")
                    final_evs.append(ev)

    S.finish(final_evs)
    es_top.close()
    return nc, outs, S


def host_constants():
    ident = np.eye(128, dtype=np.float32)
    tri = np.triu(np.ones((128, 128), np.float32))
    sel72 = np.zeros((128, 128), np.float32)
    sel72[72, :] = 1.0
    k = np.arange(128)[:, None]
    q = np.arange(BW)[None, :]
    m0 = np.where(k <= q + 8, 0.0, NEG).astype(np.float32)
    m1 = np.where(k + 128 <= q + 8, 0.0, NEG).astype(np.float32)
    KS = list(range(9)) + [8 * m for m in range(2, 17)]
    kkv = np.asarray(KS, np.float32)
    ch = np.arange(128) // 16
    rowmask8 = (ch[:, None] == np.arange(8)[None, :]).astype(np.float32)
    c2 = np.arange(128) // 64
    cm = np.zeros((128, 4, 128), np.float32)
    for qq in range(4):
        for p in range(128):
            g8 = 2 * qq + c2[p]
            cm[p, qq, g8 * 16:(g8 + 1) * 16] = 1.0
    bd = (ch[:, None] == ch[None, :]).astype(np.float32)
    return dict(c_ident=ident, c_tri=tri, c_sel72=sel72, c_mask=np.concatenate([m0, m1], axis=1),
                c_kk=kkv, c_rowmask8=rowmask8, c_cmask=cm.reshape(128, 512), c_bdmask=bd)


def make_in_maps(inputs):
    x = np.asarray(inputs["x"], np.float32)
    meta = np.asarray(inputs["meta"], np.float32)
    consts = host_constants()
    maps = []
    for c in range(8):
        b, j = divmod(c, 4)
        pad = 128 * (3 - j)
        hall = np.zeros((R_ALL, D), np.float32)
        seq = np.concatenate([meta, x[b]], axis=0)
        n = min(R_ALL - pad, seq.shape[0])
        hall[pad:pad + n] = seq[:n]
        kb = np.zeros((R_ALL,), np.float32)
        kb[:pad] = NEG
        m = dict(hall=hall, kbneg=np.ascontiguousarray(kb.reshape(NT_ALL, 128).T),
                 w_in=np.ascontiguousarray(inputs["w_in"][0], dtype=np.float32),
                 g_mix=np.ascontiguousarray(inputs["g_mix"][0], dtype=np.float32),
                 b_f=np.ascontiguousarray(inputs["b_f"][0], dtype=np.float32))
        for nm in ("lam_re", "lam_im", "log_dt", "b_re", "b_im", "c_re", "c_im", "d_skip", "w_glu", "w_attn_o",
                   "w_out", "g_ffn", "w_up", "conv_w", "conv_b", "w_down"):
            m[nm] = np.ascontiguousarray(inputs[nm][0], dtype=np.float32)
        m["g_final"] = np.ascontiguousarray(inputs["g_final"], dtype=np.float32)
        m.update(consts)
        maps.append(m)
    return maps


_CACHE = {}


def kernel(**inputs):
    if "nc" not in _CACHE:
        _CACHE["nc"] = build("full")[0]
    nc = _CACHE["nc"]
    maps = make_in_maps(inputs)
    res = run_bass_kernel_spmd(nc, maps, core_ids=list(range(8)))
    _CACHE["res"] = res
    out = np.zeros((2, 8192, D), np.float32)
    for c in range(8):
        b, j = divmod(c, 4)
        o = np.asarray(res.results[c]["out"], dtype=np.float32).reshape(NB, 128, D)
        for i in range(NB):
            G = 4 * i + j
            out[b, 128 * G:128 * (G + 1)] = o[i]
    return out
```

```python
from contextlib import ExitStack
import numpy as np
import ml_dtypes
import concourse.bass as bass
import concourse.mybir as mybir
from concourse.bass_utils import run_bass_kernel_spmd

F32 = mybir.dt.float32
BF16 = mybir.dt.bfloat16
AF = mybir.ActivationFunctionType
ALU = mybir.AluOpType

D = 2048
NT_ALL = 65
R_ALL = NT_ALL * 128
NB = 16
BW = 136
R_OWN = NB * BW
H = 8
DFF = 5632
NEG = -30000.0
OFF_Q, OFF_K, OFF_V, OFF_F, OFF_U, OFF_GA, OFF_GB = 0, 1024, 2048, 3072, 3080, 4104, 6152
N_IN = 8200
DEBUG = False


class Buf:
    __slots__ = ("w", "r")

    def __init__(self):
        self.w = None
        self.r = {}


def bufs(n):
    return [Buf() for _ in range(n)]


class Sched:
    NDS = 8

    def __init__(self, nc):
        self.nc = nc
        self.e = dict(pe=nc.tensor, act=nc.scalar, dve=nc.vector, pool=nc.gpsimd, sp=nc.sync)
        self.csem = {}
        self.ccnt = {}
        self.nsem = 0
        for k in self.e:
            self._new_csem(k)
        self.waited = {k: {} for k in self.e}
        self.dsems = {k: [] for k in self.e}
        self.dnext = {k: 0 for k in self.e}
        self.ninstr = 0

    def _alloc(self, nm):
        self.nsem += 1
        return self.nc.alloc_semaphore(name=f"{nm}_{self.nsem}")

    def _new_csem(self, k):
        self.csem[k] = self._alloc("c" + k)
        self.ccnt[k] = 0

    def _deps(self, eng, r, w):
        evs = []
        for b in r:
            if b.w is not None:
                evs.append(b.w)
        for b in w:
            if b.w is not None:
                evs.append(b.w)
            for key, ev in b.r.items():
                evs.append(ev)
        return evs

    def _wait(self, eng, evs):
        best = {}
        for ev in evs:
            h, v, src = ev
            if eng == "pe" and src == "pe":
                continue
            if best.get(h.num, (None, 0))[1] < v:
                best[h.num] = (h, v)
        wd = self.waited[eng]
        for num, (h, v) in best.items():
            if wd.get(num, 0) >= v:
                continue
            self.e[eng].wait_ge(h, v)
            self.ninstr += 1
            wd[num] = v

    def op(self, eng, fn, r=(), w=()):
        self._wait(eng, self._deps(eng, r, w))
        ins = fn(self.e[eng])
        self.ninstr += 1
        if self.ccnt[eng] >= 20000:
            self._new_csem(eng)
        self.ccnt[eng] += 1
        h = self.csem[eng]
        ins.then_inc(h, 1)
        ev = (h, self.ccnt[eng], eng)
        for b in w:
            b.w = ev
            b.r = {}
        for b in r:
            b.r[eng] = ev
        return ev

    def dma(self, out, in_, r=(), w=(), eng="sp"):
        evs = self._deps("dma", r, w)
        pool = self.dsems[eng]
        if len(pool) < self.NDS:
            pool.append([self._alloc("d" + eng), 0])
        idx = self.dnext[eng] % self.NDS
        self.dnext[eng] += 1
        slot = pool[idx]
        if slot[1] > 0:
            evs.append((slot[0], slot[1], "dma"))
        self._wait(eng, evs)
        if slot[1] >= 30000:
            slot[0] = self._alloc("d" + eng)
            slot[1] = 0
        ins = self.e[eng].dma_start(out=out, in_=in_)
        self.ninstr += 1
        slot[1] += 16
        ins.then_inc(slot[0], 16)
        ev = (slot[0], slot[1], "dma")
        for b in w:
            b.w = ev
            b.r = {}
        for b in r:
            b.r[("d", eng, idx)] = ev
        return ev

    def barrier(self):
        evs = []
        for k in self.e:
            if self.ccnt[k] > 0:
                evs.append((self.csem[k], self.ccnt[k], k))
        for eng, pool in self.dsems.items():
            for slot in pool:
                if slot[1] > 0:
                    evs.append((slot[0], slot[1], "dma"))
        for k in self.e:
            self._wait(k, [ev for ev in evs if ev[2] != k])

    def finish(self, evs):
        self._wait("sp", evs)


class PhaseStack(ExitStack):
    def __init__(self, sched):
        super().__init__()
        self._sched = sched

    def __exit__(self, *a):
        r = super().__exit__(*a)
        self._sched.barrier()
        return r

    def close(self):
        super().close()
        self._sched.barrier()


def own_row(i, q=0):
    return 128 * (4 * i + 3) + 8 + q


def build(stage="full"):
    nc = bass.Bass("TRN2", target_bir_lowering=False)
    S = Sched(nc)

    def din(name, shape, dt=F32):
        return nc.dram_tensor(name, list(shape), dt, kind="ExternalInput").ap()

    def dscr(name, shape, dt):
        return nc.dram_tensor(name, list(shape), dt, kind="Internal").ap()

    hall = din("hall", [R_ALL, D])
    kbneg = din("kbneg", [128, NT_ALL])
    w_in = din("w_in", [D, N_IN])
    g_mix = din("g_mix", [D])
    b_f = din("b_f", [H])
    c_ident = din("c_ident", [128, 128])
    c_tri = din("c_tri", [128, 128])
    c_sel72 = din("c_sel72", [128, 128])
    c_mask = din("c_mask", [128, 2 * BW])
    lam_re = din("lam_re", [64, 64])
    lam_im = din("lam_im", [64, 64])
    log_dt = din("log_dt", [64])
    b_re = din("b_re", [64, 64, 16])
    b_im = din("b_im", [64, 64, 16])
    c_re = din("c_re", [64, 16, 64])
    c_im = din("c_im", [64, 16, 64])
    d_skip = din("d_skip", [1024])
    c_kk = din("c_kk", [24])
    w_glu = din("w_glu", [1024, 4096])
    w_attn_o = din("w_attn_o", [1024, D])
    w_out = din("w_out", [D, D])
    g_ffn = din("g_ffn", [D])
    w_up = din("w_up", [D, 2 * DFF])
    conv_w = din("conv_w", [3, DFF])
    conv_b = din("conv_b", [DFF])
    w_down = din("w_down", [DFF, D])
    g_final = din("g_final", [D])
    c_rowmask8 = din("c_rowmask8", [128, 8])
    c_cmask = din("c_cmask", [128, 4 * 128])
    c_bdmask = din("c_bdmask", [128, 128])

    KT = dscr("KT", [H, 128, R_ALL], BF16)
    VV = dscr("VV", [NT_ALL, 128, 1024], BF16)
    UT = dscr("UT", [8, 128, R_ALL], BF16)
    SGA = dscr("SGA", [16, 128, R_OWN], BF16)
    SGB = dscr("SGB", [16, 128, R_OWN], BF16)
    AT = dscr("AT", [H, 128, R_OWN], BF16)
    b_KTd = [bufs(17) for _ in range(H)]
    b_UTd = [bufs(17) for _ in range(8)]
    b_VVd = bufs(NT_ALL)
    b_SGAd = bufs(16)
    b_SGBd = bufs(16)
    b_ATd = bufs(H)

    outs = {}

    def dout(name, shape, dt=F32):
        t = nc.dram_tensor(name, list(shape), dt, kind="ExternalOutput").ap()
        outs[name] = t
        return t

    final_evs = []
    es_top = ExitStack()
    es_top.enter_context(nc.allow_non_contiguous_dma(reason="small strided parameter / layout DMAs"))

    uid = [0]

    def sb(es, name, shape, dt):
        uid[0] += 1
        return es.enter_context(nc.sbuf_tensor(f"{name}_{uid[0]}", list(shape), dt))

    PS = [es_top.enter_context(nc.psum_tensor(f"ps{i}", [128, 512], F32)) for i in range(8)]
    PSB = bufs(8)

    ident_f = sb(es_top, "ident_f", [128, 128], F32)
    ident_b = sb(es_top, "ident_b", [128, 128], BF16)
    ones_b = sb(es_top, "ones_b", [128, 128], BF16)
    ones_f = sb(es_top, "ones_f", [128, 128], F32)
    tri_f = sb(es_top, "tri_f", [128, 128], F32)
    sel72_f = sb(es_top, "sel72_f", [128, 128], F32)
    mask_f = sb(es_top, "mask_f", [128, 2 * BW], F32)
    mask_b = sb(es_top, "mask_b", [128, 2 * BW], BF16)
    G1 = sb(es_top, "G1", [128, 16], F32)
    cst = sb(es_top, "cst", [128, 8], F32)
    NBt = sb(es_top, "NBt", [128, NT_ALL, H], F32)
    NREF = sb(es_top, "NREF", [128, NB, H], F32)
    b_const = Buf()
    b_NB = Buf()
    b_NREF = Buf()
    b_QT = bufs(H)

    S.dma(ident_f[:], c_ident[:, :], w=[b_const])
    S.dma(tri_f[:], c_tri[:, :], w=[b_const])
    S.dma(sel72_f[:], c_sel72[:, :], w=[b_const])
    S.dma(mask_f[:], c_mask[:, :], w=[b_const])
    S.dma(G1[:], g_mix.rearrange("(t p) -> p t", p=128), w=[b_const])
    S.op("dve", lambda e: e.tensor_copy(out=ident_b[:], in_=ident_f[:]), r=[b_const], w=[b_const])
    S.op("dve", lambda e: e.tensor_copy(out=mask_b[:], in_=mask_f[:]), r=[b_const], w=[b_const])
    S.op("pool", lambda e: e.memset(ones_b[:], 1.0), w=[b_const])
    S.op("pool", lambda e: e.memset(ones_f[:], 1.0), w=[b_const])
    S.op("pool", lambda e: e.memset(cst[:, 0:1], 1e-6), w=[b_const])
    S.op("pool", lambda e: e.memset(cst[:, 1:2], -0.5), w=[b_const])
    S.op("pool", lambda e: e.memset(cst[:, 2:3], 1.0), w=[b_const])

    def norm_transpose(es_bufs, xt, b_xt, nrows, dst_fn, b_dst, ps_pair):
        junk, ssq, ms, rstd, xs, b_tmp, b_xs = es_bufs
        S.op("act", lambda e: e.activation(out=junk[0:nrows, :], in_=xt[0:nrows, :], func=AF.Square),
             r=[b_xt], w=[b_tmp])
        w_ = D // 2
        while w_ >= 1:
            S.op("dve", lambda e, w_=w_: e.tensor_tensor(out=junk[0:nrows, 0:w_], in0=junk[0:nrows, 0:w_],
                                                         in1=junk[0:nrows, w_:2 * w_], op=ALU.add),
                 r=[b_tmp], w=[b_tmp])
            w_ //= 2
        ssq = junk
        S.op("act", lambda e: e.activation(out=ms[0:nrows, :], in_=ssq[0:nrows, 0:1], func=AF.Sqrt, scale=1.0 / D,
                                           bias=cst[0:nrows, 0:1]), r=[b_tmp, b_const], w=[b_tmp])
        S.op("dve", lambda e: e.reciprocal(out=rstd[0:nrows, :], in_=ms[0:nrows, :]), r=[b_tmp], w=[b_tmp])
        S.op("dve", lambda e: e.tensor_scalar(out=xs[0:nrows, :], in0=xt[0:nrows, :], scalar1=rstd[0:nrows, 0:1],
                                              scalar2=None, op0=ALU.mult), r=[b_tmp, b_xt], w=[b_xs])
        for half in range(2):
            pi = ps_pair[half]
            pv = PS[pi][:].bitcast(BF16)
            for j in range(8):
                dt_ = half * 8 + j
                S.op("pe", lambda e, dt_=dt_, j=j: e.transpose(out=pv[:, j * 128: j * 128 + nrows],
                                                               in_=xs[0:nrows, dt_ * 128:(dt_ + 1) * 128],
                                                               identity=ident_b[0:nrows, 0:nrows]),
                     r=[b_xs, b_const], w=[PSB[pi]])
            src = pv.rearrange("p (j c) -> p j c", c=128)[:, :, 0:nrows]
            eng = "act" if half == 0 else "dve"
            if eng == "act":
                S.op("act", lambda e, src=src, half=half: e.copy(out=dst_fn(half), in_=src), r=[PSB[pi]], w=[b_dst[half]])
            else:
                S.op("dve", lambda e, src=src, half=half: e.tensor_copy(out=dst_fn(half), in_=src), r=[PSB[pi]], w=[b_dst[half]])

    def cast_scaled(eng, out, in_, scale_ap, r, w):
        if eng == "act":
            return S.op("act", lambda e: e.activation(out=out, in_=in_, func=AF.Identity, scale=scale_ap), r=r, w=w)
        return S.op("dve", lambda e: e.tensor_scalar(out=out, in0=in_, scalar1=scale_ap, scalar2=None, op0=ALU.mult), r=r, w=w)

    def cast_plain(eng, out, in_, r, w):
        if eng == "act":
            return S.op("act", lambda e: e.copy(out=out, in_=in_), r=r, w=w)
        return S.op("dve", lambda e: e.tensor_copy(out=out, in_=in_), r=r, w=w)

    with PhaseStack(S) as es:
        WK = sb(es, "WKVU", [128, 16, 3080], BF16)
        b_WK = bufs(16)
        stg = [sb(es, f"wstg{i}", [128, 1024], F32) for i in range(2)]
        b_stg = bufs(2)
        fstg = sb(es, "fstg", [128, 16, 8], F32)
        b_fstg = Buf()
        S.dma(fstg[:], w_in[:, OFF_F:OFF_F + 8].rearrange("(t p) c -> p t c", p=128), w=[b_fstg])
        k = 0
        for dt_ in range(16):
            for pi, off in enumerate((OFF_K, OFF_V, OFF_U)):
                sl = k % 2
                k += 1
                S.dma(stg[sl][:], w_in[dt_ * 128:(dt_ + 1) * 128, off:off + 1024], w=[b_stg[sl]])
                cast_scaled("act" if k % 2 else "dve", WK[:, dt_, pi * 1024:(pi + 1) * 1024], stg[sl][:],
                            G1[:, dt_:dt_ + 1], [b_stg[sl], b_const], [b_WK[dt_]])
            cast_scaled("dve", WK[:, dt_, 3072:3080], fstg[:, dt_, :], G1[:, dt_:dt_ + 1], [b_fstg, b_const], [b_WK[dt_]])

        xts = [sb(es, f"xt{i}", [128, D], F32) for i in range(2)]
        b_xts = bufs(2)
        junk = sb(es, "junk", [128, D], F32)
        ssq = sb(es, "ssq", [128, 1], F32)
        ms = sb(es, "ms", [128, 1], F32)
        rstd = sb(es, "rstd", [128, 1], F32)
        xs = sb(es, "xs", [128, D], BF16)
        nb_ = (junk, ssq, ms, rstd, xs, Buf(), Buf())
        nT = [sb(es, f"nT{i}", [128, 16, 512], BF16) for i in range(2)]
        b_nT = [[bufs(2) for _ in range(4)] for _ in range(2)]
        kst = [sb(es, f"kst{i}", [128, 512], BF16) for i in range(2)]
        b_kst = bufs(2)
        vst = [sb(es, f"vst{i}", [128, 1024], BF16) for i in range(2)]
        b_vst = bufs(2)
        Fraw = sb(es, "Fraw", [128, NT_ALL, H], F32)
        b_Fraw = Buf()

        ngroups = (NT_ALL + 3) // 4
        kcount = 0
        vcount = 0
        gcount = 0
        for gi in range(ngroups):
            tiles = list(range(gi * 4, min(gi * 4 + 4, NT_ALL)))
            gb = gi % 2
            N = 128 * len(tiles)
            for tc, t in enumerate(tiles):
                sl = t % 2
                S.dma(xts[sl][:], hall[t * 128:(t + 1) * 128, :], w=[b_xts[sl]])
                norm_transpose(nb_, xts[sl], b_xts[sl], 128,
                               lambda half, tc=tc, gb=gb: nT[gb][:, half * 8:(half + 1) * 8, tc * 128:(tc + 1) * 128],
                               b_nT[gb][tc], (0, 1))
            rdeps = [b for tc in range(len(tiles)) for b in b_nT[gb][tc]]
            for which, coff, dst in (("k", 0, KT), ("u", 2048, UT)):
                for c in range(8):
                    pi = 2 + (gcount % 3)
                    gcount += 1
                    for dt_ in range(16):
                        S.op("pe", lambda e, dt_=dt_, c=c, coff=coff, pi=pi: e.matmul(
                            PS[pi][:, 0:N], lhsT=WK[:, dt_, coff + c * 128: coff + (c + 1) * 128],
                            rhs=nT[gb][:, dt_, 0:N], start=(dt_ == 0), stop=(dt_ == 15)),
                            r=rdeps + [b_WK[dt_]], w=[PSB[pi]])
                    ks = kcount % 2
                    kcount += 1
                    if kcount % 2 == 0:
                        S.op("act", lambda e, ks=ks, pi=pi: e.copy(out=kst[ks][:, 0:N], in_=PS[pi][:, 0:N]),
                             r=[PSB[pi]], w=[b_kst[ks]])
                    else:
                        S.op("dve", lambda e, ks=ks, pi=pi: e.tensor_copy(out=kst[ks][:, 0:N], in_=PS[pi][:, 0:N]),
                             r=[PSB[pi]], w=[b_kst[ks]])
                    S.dma(dst[c, :, gi * 512: gi * 512 + N], kst[ks][:, 0:N], r=[b_kst[ks]],
                          w=[(b_KTd if which == "k" else b_UTd)[c][gi]], eng="pool")
            for tc, t in enumerate(tiles):
                vs = vcount % 2
                vcount += 1
                for half in range(2):
                    pi = 2 + (gcount % 3)
                    gcount += 1
                    for dt_ in range(16):
                        S.op("pe", lambda e, dt_=dt_, tc=tc, half=half, pi=pi: e.matmul(
                            PS[pi][:, :], lhsT=nT[gb][:, dt_, tc * 128:(tc + 1) * 128],
                            rhs=WK[:, dt_, 1024 + half * 512: 1024 + (half + 1) * 512],
                            start=(dt_ == 0), stop=(dt_ == 15)),
                            r=b_nT[gb][tc] + [b_WK[dt_]], w=[PSB[pi]])
                    if half == 0:
                        S.op("act", lambda e, vs=vs, pi=pi, half=half: e.copy(
                            out=vst[vs][:, half * 512:(half + 1) * 512], in_=PS[pi][:, :]),
                            r=[PSB[pi]], w=[b_vst[vs]])
                    else:
                        S.op("dve", lambda e, vs=vs, pi=pi, half=half: e.tensor_copy(
                            out=vst[vs][:, half * 512:(half + 1) * 512], in_=PS[pi][:, :]),
                            r=[PSB[pi]], w=[b_vst[vs]])
                S.dma(VV[t, :, :], vst[vs][:], r=[b_vst[vs]], w=[b_VVd[t]], eng="pool")
                pi = 5
                for dt_ in range(16):
                    S.op("pe", lambda e, dt_=dt_, tc=tc, pi=pi: e.matmul(
                        PS[pi][:, 0:8], lhsT=nT[gb][:, dt_, tc * 128:(tc + 1) * 128], rhs=WK[:, dt_, 3072:3080],
                        start=(dt_ == 0), stop=(dt_ == 15)), r=b_nT[gb][tc] + [b_WK[dt_]], w=[PSB[pi]])
                S.op("dve", lambda e, t=t, pi=pi: e.tensor_copy(out=Fraw[:, t, :], in_=PS[pi][:, 0:8]),
                     r=[PSB[pi]], w=[b_Fraw])

        bfB = sb(es, "bfB", [128, 1, H], F32)
        kbt = sb(es, "kbt", [128, NT_ALL, 1], F32)
        Fx = sb(es, "Fx", [128, NT_ALL, H], F32)
        TOT = sb(es, "TOT", [128, NT_ALL, H], F32)
        CUM = sb(es, "CUM", [128, NT_ALL, H], F32)
        b_F = Buf()
        b_F2 = Buf()
        S.dma(bfB[:, 0, :], b_f.partition_broadcast(128), w=[b_F])
        S.dma(kbt[:, :, 0], kbneg[:, :], w=[b_F])
        S.op("dve", lambda e: e.tensor_tensor(out=Fx[:], in0=Fraw[:], in1=bfB[:].broadcast_to([128, NT_ALL, H]),
                                              op=ALU.add), r=[b_Fraw, b_F], w=[b_F2])
        S.op("act", lambda e: e.activation(out=Fx[:], in_=Fx[:], func=AF.Exp, scale=-1.0), r=[b_F2], w=[b_F2])
        S.op("act", lambda e: e.activation(out=Fx[:], in_=Fx[:], func=AF.Ln, bias=cst[:, 2:3], scale=1.0),
             r=[b_F2, b_const], w=[b_F2])
        Fx2 = Fx[:].rearrange("p t h -> p (t h)")
        halves = [(0, 264), (264, 520)]
        for (a0, a1) in halves:
            S.op("pe", lambda e, a0=a0, a1=a1: e.matmul(PS[6][:, 0:a1 - a0], lhsT=tri_f[:], rhs=Fx2[:, a0:a1],
                                                        start=True, stop=True), r=[b_F2, b_const], w=[PSB[6]])
            S.op("dve", lambda e, a0=a0, a1=a1: e.tensor_copy(
                out=NBt[:].rearrange("p t h -> p (t h)")[:, a0:a1], in_=PS[6][:, 0:a1 - a0]), r=[PSB[6]], w=[b_NB])
            S.op("pe", lambda e, a0=a0, a1=a1: e.matmul(PS[7][:, 0:a1 - a0], lhsT=ones_f[:], rhs=Fx2[:, a0:a1],
                                                        start=True, stop=True), r=[b_F2, b_const], w=[PSB[7]])
            S.op("dve", lambda e, a0=a0, a1=a1: e.tensor_copy(
                out=TOT[:].rearrange("p t h -> p (t h)")[:, a0:a1], in_=PS[7][:, 0:a1 - a0]), r=[PSB[7]], w=[b_F])
        for h in range(H):
            S.op("dve", lambda e, h=h: e.tensor_tensor_scan(
                out=CUM[:, :, h], data0=ones_f[:, 0:NT_ALL], data1=TOT[:, :, h], initial=0.0,
                op0=ALU.mult, op1=ALU.add), r=[b_F, b_const], w=[b_F])
        S.op("dve", lambda e: e.tensor_tensor(out=CUM[:], in0=CUM[:], in1=TOT[:], op=ALU.subtract), r=[b_F], w=[b_F])
        S.op("dve", lambda e: e.tensor_tensor(out=NBt[:], in0=NBt[:], in1=CUM[:], op=ALU.add), r=[b_F, b_NB], w=[b_NB])
        S.op("pe", lambda e: e.matmul(PS[6][:, 0:NB * H].rearrange("p (i h) -> p i h", h=H), lhsT=sel72_f[:],
                                      rhs=NBt[:, 3:NT_ALL:4, :][:, 0:NB, :], start=True, stop=True),
             r=[b_NB, b_const], w=[PSB[6]])
        S.op("dve", lambda e: e.tensor_copy(out=NREF[:].rearrange("p i h -> p (i h)"), in_=PS[6][:, 0:NB * H]),
             r=[PSB[6]], w=[b_NREF])
        S.op("dve", lambda e: e.tensor_tensor(out=NBt[:], in0=NBt[:],
                                              in1=kbt[:].broadcast_to([128, NT_ALL, H]), op=ALU.add),
             r=[b_F, b_NB, PSB[6]], w=[b_NB])
        if stage == "A":
            dF = dout("dbgF", [128, NT_ALL * H])
            final_evs.append(S.dma(dF[:, :], NBt[:].rearrange("p t h -> p (t h)"), r=[b_NB]))
            dR = dout("dbgR", [128, NB * H])
            final_evs.append(S.dma(dR[:, :], NREF[:].rearrange("p i h -> p (i h)"), r=[b_NREF]))

    es_q = PhaseStack(S)
    QT = sb(es_q, "QT", [128, H, R_OWN], BF16)
    if stage != "C":
        with PhaseStack(S) as es:
            nO = sb(es, "nO", [128, 16, R_OWN], BF16)
            b_nO = [[bufs(2) for _ in range(2)] for _ in range(NB)]
            xts = [sb(es, f"xo{i}", [128, D], F32) for i in range(2)]
            b_xts = bufs(2)
            junk = sb(es, "junk", [128, D], F32)
            ssq = sb(es, "ssq", [128, 1], F32)
            ms = sb(es, "ms", [128, 1], F32)
            rstd = sb(es, "rstd", [128, 1], F32)
            xs = sb(es, "xs", [128, D], BF16)
            nb_ = (junk, ssq, ms, rstd, xs, Buf(), Buf())
            cnt = 0
            for i in range(NB):
                for part, (r0, nr) in enumerate(((0, 128), (128, 8))):
                    sl = cnt % 2
                    cnt += 1
                    S.dma(xts[sl][0:nr, :], hall[own_row(i, r0): own_row(i, r0) + nr, :], w=[b_xts[sl]])
                    c0 = i * BW + r0
                    norm_transpose(nb_, xts[sl], b_xts[sl], nr,
                                   lambda half, c0=c0, nr=nr: nO[:, half * 8:(half + 1) * 8, c0:c0 + nr],
                                   b_nO[i][part], (0, 1))
            all_nO = [b for i in range(NB) for part in range(2) for b in b_nO[i][part]]
            wst = [sb(es, f"wst{i}", [128, 16, 256], F32) for i in range(2)]
            b_wst = bufs(2)
            wbf = [sb(es, f"wbf{i}", [128, 16, 256], BF16) for i in range(2)]
            b_wbf = bufs(2)
            gst = [sb(es, f"gst{i}", [128, R_OWN], BF16) for i in range(2)]
            b_gst = bufs(2)
            ntiles = [(0, 512), (512, 1024), (1024, 1536), (1536, 2048), (2048, 2176)]
            chunks = [("q", OFF_Q + 256 * c, c) for c in range(4)] + \
                     [("ga", OFF_GA + 256 * c, c) for c in range(8)] + [("gb", OFF_GB + 256 * c, c) for c in range(8)]
            gcount = 0
            gsc = 0
            def p1b_load(ci_):
                S.dma(wst[ci_ % 2][:], w_in[:, chunks[ci_][1]:chunks[ci_][1] + 256].rearrange("(t p) c -> p t c", p=128),
                      w=[b_wst[ci_ % 2]])

            def p1b_cast(ci_):
                for dt_ in range(16):
                    cast_scaled("act" if dt_ % 2 else "dve", wbf[ci_ % 2][:, dt_, :], wst[ci_ % 2][:, dt_, :],
                                G1[:, dt_:dt_ + 1], [b_wst[ci_ % 2], b_const], [b_wbf[ci_ % 2]])

            p1b_load(0)
            p1b_cast(0)
            for ci, (kind, off, c) in enumerate(chunks):
                sl = ci % 2
                if ci + 1 < len(chunks):
                    p1b_load(ci + 1)
                for sub in range(2):
                    if sub == 1 and ci + 1 < len(chunks):
                        p1b_cast(ci + 1)
                    ct = c * 2 + sub
                    if kind != "q":
                        gs = gsc % 2
                        gsc += 1
                    for (n0, n1) in ntiles:
                        pi = 2 + (gcount % 4)
                        gcount += 1
                        for dt_ in range(16):
                            S.op("pe", lambda e, dt_=dt_, sl=sl, sub=sub, pi=pi, n0=n0, n1=n1: e.matmul(
                                PS[pi][:, 0:n1 - n0], lhsT=wbf[sl][:, dt_, sub * 128:(sub + 1) * 128],
                                rhs=nO[:, dt_, n0:n1], start=(dt_ == 0), stop=(dt_ == 15)),
                                r=all_nO + [b_wbf[sl]], w=[PSB[pi]])
                        if kind == "q":
                            S.op("dve", lambda e, ct=ct, pi=pi, n0=n0, n1=n1: e.tensor_copy(
                                out=QT[:, ct, n0:n1], in_=PS[pi][:, 0:n1 - n0]), r=[PSB[pi]], w=[b_QT[ct]])
                        else:
                            S.op("act", lambda e, gs=gs, pi=pi, n0=n0, n1=n1: e.activation(
                                out=gst[gs][:, n0:n1], in_=PS[pi][:, 0:n1 - n0], func=AF.Sigmoid),
                                r=[PSB[pi]], w=[b_gst[gs]])
                    if kind != "q":
                        dst = SGA if kind == "ga" else SGB
                        S.dma(dst[ct, :, :], gst[gs][:], r=[b_gst[gs]], w=[(b_SGAd if kind == "ga" else b_SGBd)[ct]], eng="pool")
            if stage == "A":
                dQ = dout("dbgQ", [128, H * R_OWN], BF16)
                final_evs.append(S.dma(dQ[:, :], QT[:].rearrange("p h r -> p (h r)"), r=b_QT))

    if stage == "A":
        dK = dout("dbgK", [128, R_ALL], BF16)
        final_evs.append(S.dma(dK[:, :], KT[3, :, :], r=b_KTd[3]))
        dV = dout("dbgV", [128, 1024], BF16)
        final_evs.append(S.dma(dV[:, :], VV[7, :, :], r=[b_VVd[7]]))
        dU = dout("dbgU", [128, R_ALL], BF16)
        final_evs.append(S.dma(dU[:, :], UT[5, :, :], r=b_UTd[5]))
        dG = dout("dbgG", [128, R_OWN], BF16)
        final_evs.append(S.dma(dG[:, :], SGB[9, :, :], r=[b_SGBd[9]]))

    if stage in ("B", "full"):
        with PhaseStack(S) as es:
            KTs = [sb(es, f"KTs{i}", [128, R_ALL], BF16) for i in range(2)]
            Vs = [sb(es, f"Vs{i}", [128, NT_ALL, 128], BF16) for i in range(2)]
            b_KV = bufs(2)
            bias = [sb(es, f"bias{i}", [128, NT_ALL], F32) for i in range(2)]
            b_bias = bufs(2)
            PT = [sb(es, f"PT{i}", [128, BW], BF16) for i in range(4)]
            b_PT = bufs(4)
            rec = sb(es, "rec", [128, BW], F32)
            b_rec = Buf()
            ast = [sb(es, f"ast{i}", [128, R_OWN], BF16) for i in range(2)]
            b_ast = bufs(2)
            scale = 128.0 ** -0.5
            bcount = 0
            pcount = 0
            scount = 0
            for h in range(H):
                hb = h % 2
                S.dma(KTs[hb][:], KT[h, :, :], r=b_KTd[h], w=[b_KV[hb]])
                S.dma(Vs[hb][:], VV[:, :, h * 128:(h + 1) * 128].rearrange("t p c -> p t c"), r=b_VVd, w=[b_KV[hb]])
                for i in range(NB):
                    nk = 4 * i + 5
                    bb = bcount % 2
                    bcount += 1
                    S.op("dve", lambda e, bb=bb, nk=nk, i=i, h=h: e.tensor_scalar(
                        out=bias[bb][:, 0:nk], in0=NBt[:, 0:nk, h], scalar1=NREF[:, i, h:h + 1], scalar2=None,
                        op0=ALU.subtract), r=[b_NB, b_NREF], w=[b_bias[bb]])
                    q_ap = QT[:, h, i * BW:(i + 1) * BW]
                    PO, PL = 6, 7

                    def qk(kt):
                        nonlocal scount
                        pi = scount % 4
                        scount += 1
                        masked = kt >= nk - 2
                        S.op("pe", lambda e: e.matmul(PS[pi][:, 0:BW], lhsT=KTs[hb][:, kt * 128:(kt + 1) * 128],
                                                      rhs=q_ap, start=True, stop=not masked),
                             r=[b_KV[hb], b_QT[h]], w=[PSB[pi]])
                        if masked:
                            mi = kt - (nk - 2)
                            S.op("pe", lambda e: e.matmul(PS[pi][:, 0:BW], lhsT=ident_b[:],
                                                          rhs=mask_b[:, mi * BW:(mi + 1) * BW], start=False, stop=True),
                                 r=[b_const], w=[PSB[pi]])
                        return pi

                    pq = []
                    nxt_kt = 0
                    for kt in range(nk):
                        while nxt_kt < nk and len(pq) < 3:
                            pq.append(qk(nxt_kt))
                            nxt_kt += 1
                        pi = pq.pop(0)
                        ps_ = pcount % 4
                        pcount += 1
                        S.op("act", lambda e, pi=pi, ps_=ps_, kt=kt: e.activation(
                            out=PT[ps_][:], in_=PS[pi][:, 0:BW], func=AF.Exp, bias=bias[bb][:, kt:kt + 1], scale=scale),
                            r=[PSB[pi], b_bias[bb]], w=[b_PT[ps_]])
                        S.op("pe", lambda e, ps_=ps_, kt=kt: e.matmul(
                            PS[PO][:, 0:BW], lhsT=Vs[hb][:, kt, :], rhs=PT[ps_][:], start=(kt == 0), stop=(kt == nk - 1)),
                            r=[b_KV[hb], b_PT[ps_]], w=[PSB[PO]])
                        S.op("pe", lambda e, ps_=ps_, kt=kt: e.matmul(
                            PS[PL][:, 0:BW], lhsT=ones_b[:], rhs=PT[ps_][:], start=(kt == 0), stop=(kt == nk - 1)),
                            r=[b_const, b_PT[ps_]], w=[PSB[PL]])
                    S.op("dve", lambda e: e.reciprocal(out=rec[:], in_=PS[PL][:, 0:BW]), r=[PSB[PL]], w=[b_rec])
                    S.op("dve", lambda e, i=i: e.tensor_tensor(out=ast[hb][:, i * BW:(i + 1) * BW], in0=PS[PO][:, 0:BW],
                                                               in1=rec[:], op=ALU.mult),
                         r=[PSB[PO], b_rec], w=[b_ast[hb]])
                S.dma(AT[h, :, :], ast[hb][:], r=[b_ast[hb]], w=[b_ATd[h]], eng="pool")
        if stage == "B":
            dA = dout("dbgA", [H, 128, R_OWN], BF16)
            final_evs.append(S.dma(dA[:, :, :], AT[:, :, :], r=b_ATd))


    es_q.close()

    def xap(tile_ap, off, dims):
        return bass.AP(tile_ap.tensor, tile_ap.offset + off, [list(tile_ap.ap[0])] + [list(d) for d in dims])

    def tt(eng, out, in0, in1, op, r, w):
        return S.op(eng, lambda e: e.tensor_tensor(out=out, in0=in0, in1=in1, op=op), r=r, w=w)

    def ts(eng, out, in0, s1, s2, op0, op1, r, w):
        if op1 is None:
            return S.op(eng, lambda e: e.tensor_scalar(out=out, in0=in0, scalar1=s1, scalar2=None, op0=op0), r=r, w=w)
        return S.op(eng, lambda e: e.tensor_scalar(out=out, in0=in0, scalar1=s1, scalar2=s2, op0=op0, op1=op1), r=r, w=w)

    def stt(out, in0, sc, in1, op0, op1, r, w):
        return S.op("dve", lambda e: e.scalar_tensor_tensor(out=out, in0=in0, scalar=sc, in1=in1, op0=op0, op1=op1), r=r, w=w)

    def cp(eng, out, in_, r, w):
        if eng == "act":
            return S.op("act", lambda e: e.copy(out=out, in_=in_), r=r, w=w)
        return S.op(eng, lambda e: e.tensor_copy(out=out, in_=in_), r=r, w=w)

    def actf(out, in_, func, r, w, **kw):
        return S.op("act", lambda e: e.activation(out=out, in_=in_, func=func, **kw), r=r, w=w)

    if stage in ("C", "full"):
        KS = list(range(9)) + [8 * m for m in range(2, 17)]
        KI = {k: idx for idx, k in enumerate(KS)}
        NK = len(KS)
        PI_ = float(np.pi)
        WSM = dscr("WSM", [8, 128, 8, 4, 2, 128], BF16)
        CYM = dscr("CYM", [8, 128, 8, 4, 2, 128], BF16)
        KBM = dscr("KBM", [8, 128, 8, 128], BF16)
        YG = dscr("YG", [8, 128, R_OWN], BF16)
        b_WSMd = bufs(8)
        b_CYMd = bufs(8)
        b_KBMd = bufs(8)
        b_YGd = bufs(8)
        es_ssm = PhaseStack(S)
        AR = sb(es_ssm, "AR", [128, 16, 32, 1], F32)
        AI = sb(es_ssm, "AI", [128, 16, 32, 1], F32)
        b_A = Buf()
        with PhaseStack(S) as es:
            lamr = sb(es, "lamr", [128, 64], F32)
            lami = sb(es, "lami", [128, 64], F32)
            ldt = sb(es, "ldt", [128, 64], F32)
            bre = sb(es, "bre", [128, 64, 16], F32)
            bim = sb(es, "bim", [128, 64, 16], F32)
            Cre = sb(es, "Cre", [128, 64, 16], F32)
            Cim = sb(es, "Cim", [128, 64, 16], F32)
            cn = [sb(es, f"cn{i}", [128, 128], F32) for i in range(2)]
            b_cn = bufs(2)
            kk = sb(es, "kk", [128, NK, 1], F32)
            rm8 = sb(es, "rm8", [128, 8], F32)
            cmask = sb(es, "cmask", [128, 4, 128], F32)
            bdm = sb(es, "bdm", [128, 128], F32)
            dsk = sb(es, "dsk", [128, 8], F32)
            bP = Buf()
            bC = Buf()
            for hlf in range(2):
                ps_ = slice(64 * hlf, 64 * hlf + 64)
                S.dma(lamr[ps_, :], lam_re.rearrange("g p -> p g"), w=[bP])
                S.dma(lami[ps_, :], lam_im.rearrange("g p -> p g"), w=[bP])
                S.dma(bre[ps_, :, :], b_re.rearrange("g p c -> p g c"), w=[bP])
                S.dma(bim[ps_, :, :], b_im.rearrange("g p c -> p g c"), w=[bP])
            S.dma(ldt[:], log_dt.partition_broadcast(128), w=[bP])
            S.dma(kk[:, :, 0], c_kk.partition_broadcast(128), w=[bP])
            S.dma(rm8[:], c_rowmask8[:, :], w=[bP])
            S.dma(cmask[:].rearrange("p q c -> p (q c)"), c_cmask[:, :], w=[bP])
            S.dma(bdm[:], c_bdmask[:, :], w=[bP])
            S.dma(dsk[:], d_skip.rearrange("(t p) -> p t", p=128), w=[bP])
            k = 0
            for (src, dstt) in ((c_re, Cre), (c_im, Cim)):
                dflat = dstt[:].rearrange("p g c -> p (g c)")
                for t8 in range(8):
                    sl = k % 2
                    k += 1
                    nat = src[8 * t8: 8 * t8 + 8, :, :].rearrange("g c p -> (g c) p")
                    S.dma(cn[sl][:, 0:64], nat, w=[b_cn[sl]])
                    S.dma(cn[sl][:, 64:128], nat, w=[b_cn[sl]])
                    pi = 6 + (k % 2)
                    S.op("pe", lambda e: e.transpose(out=PS[pi][:, 0:128], in_=cn[sl][:, :], identity=ident_f[:]),
                         r=[b_cn[sl], b_const], w=[PSB[pi]])
                    cp("dve", dflat[:, t8 * 128:(t8 + 1) * 128], PS[pi][:, 0:128], [PSB[pi]], [bC])
            dtt = sb(es, "dtt", [128, 64], F32)
            lrdt = sb(es, "lrdt", [128, 1, 64], F32)
            lidt = sb(es, "lidt", [128, 1, 64], F32)
            actf(dtt[:], ldt[:], AF.Exp, [bP], [bP])
            tt("dve", lrdt[:, 0, :], lamr[:], dtt[:], ALU.mult, [bP], [bP])
            tt("dve", lidt[:, 0, :], lami[:], dtt[:], ALU.mult, [bP], [bP])
            ANG = sb(es, "ANG", [128, NK, 64], F32)
            MAG = sb(es, "MAG", [128, NK, 64], F32)
            PRt = sb(es, "PRt", [128, NK, 64, 1], F32)
            PIt = sb(es, "PIt", [128, NK, 64, 1], F32)
            T1 = sb(es, "T1", [128, NK * 64], F32)
            T2 = sb(es, "T2", [128, NK * 64], F32)
            T3 = sb(es, "T3", [128, NK * 64], F32)
            TI = sb(es, "TI", [128, NK * 64], mybir.dt.int32)
            kkb = kk[:].broadcast_to([128, NK, 64])
            tt("dve", ANG[:], kkb, lidt[:].broadcast_to([128, NK, 64]), ALU.mult, [bP], [bP])
            tt("dve", MAG[:], kkb, lrdt[:].broadcast_to([128, NK, 64]), ALU.mult, [bP], [bP])
            actf(MAG[:], MAG[:], AF.Exp, [bP], [bP])
            angf = ANG[:].rearrange("p k g -> p (k g)")
            C1 = 6.28125
            C2 = 2.0 * np.pi - 6.28125

            def range_reduce(shift, dst):
                ts("dve", T1[:], angf, shift, 1.0 / (2 * np.pi), ALU.add, ALU.mult, [bP], [bP])
                cp("dve", TI[:], T1[:], [bP], [bP])
                cp("dve", T2[:], TI[:], [bP], [bP])
                ts("dve", T1[:], angf, shift, None, ALU.add, None, [bP], [bP])
                stt(T3[:], T2[:], -C1, T1[:], ALU.mult, ALU.add, [bP], [bP])
                stt(T1[:], T2[:], -float(C2), T3[:], ALU.mult, ALU.add, [bP], [bP])
                ts("dve", T2[:], T1[:], PI_, -2 * PI_, ALU.is_gt, ALU.mult, [bP], [bP])
                tt("dve", T1[:], T1[:], T2[:], ALU.add, [bP], [bP])
                ts("dve", T2[:], T1[:], -PI_, 2 * PI_, ALU.is_lt, ALU.mult, [bP], [bP])
                tt("dve", dst, T1[:], T2[:], ALU.add, [bP], [bP])

            SINt = sb(es, "SINt", [128, NK * 64], F32)
            COSt = sb(es, "COSt", [128, NK * 64], F32)
            range_reduce(0.0, SINt[:])
            range_reduce(PI_ / 2, COSt[:])
            actf(SINt[:], SINt[:], AF.Sin, [bP], [bP])
            actf(COSt[:], COSt[:], AF.Sin, [bP], [bP])
            magf = MAG[:].rearrange("p k g -> p (k g)")
            tt("dve", PRt[:].rearrange("p k g o -> p (k g o)"), magf, COSt[:], ALU.mult, [bP], [bP])
            tt("dve", PIt[:].rearrange("p k g o -> p (k g o)"), magf, SINt[:], ALU.mult, [bP], [bP])
            for hlf in range(2):
                ps_ = slice(64 * hlf, 64 * hlf + 64)
                cp("dve", AR[ps_, :, :, :], PRt[ps_, 8:24, hlf:64:2, :], [bP], [b_A])
                cp("dve", AI[ps_, :, :, :], PIt[ps_, 8:24, hlf:64:2, :], [bP], [b_A])
            nr = sb(es, "nr", [128, 64], F32)
            den = sb(es, "den", [128, 64], F32)
            tq = sb(es, "tq", [128, 64], F32)
            zr = sb(es, "zr", [128, 64, 1], F32)
            zi = sb(es, "zi", [128, 64, 1], F32)
            p1r = PRt[:, 1, :, 0]
            p1i = PIt[:, 1, :, 0]
            ts("dve", nr[:], p1r, -1.0, None, ALU.add, None, [bP], [bP])
            tt("dve", den[:], lamr[:], lamr[:], ALU.mult, [bP], [bP])
            tt("dve", tq[:], lami[:], lami[:], ALU.mult, [bP], [bP])
            tt("dve", den[:], den[:], tq[:], ALU.add, [bP], [bP])
            S.op("dve", lambda e: e.reciprocal(out=den[:], in_=den[:]), r=[bP], w=[bP])
            tt("dve", zr[:, :, 0], nr[:], lamr[:], ALU.mult, [bP], [bP])
            tt("dve", tq[:], p1i, lami[:], ALU.mult, [bP], [bP])
            tt("dve", zr[:, :, 0], zr[:, :, 0], tq[:], ALU.add, [bP], [bP])
            tt("dve", zr[:, :, 0], zr[:, :, 0], den[:], ALU.mult, [bP], [bP])
            tt("dve", zi[:, :, 0], p1i, lamr[:], ALU.mult, [bP], [bP])
            tt("dve", tq[:], nr[:], lami[:], ALU.mult, [bP], [bP])
            tt("dve", zi[:, :, 0], zi[:, :, 0], tq[:], ALU.subtract, [bP], [bP])
            tt("dve", zi[:, :, 0], zi[:, :, 0], den[:], ALU.mult, [bP], [bP])
            Bbr = sb(es, "Bbr", [128, 64, 16], F32)
            Bbi = sb(es, "Bbi", [128, 64, 16], F32)
            W1 = sb(es, "W1", [128, 64, 16], F32)
            W2 = sb(es, "W2", [128, 64, 16], F32)
            zrb = zr[:].broadcast_to([128, 64, 16])
            zib = zi[:].broadcast_to([128, 64, 16])
            tt("dve", Bbr[:], bre[:], zrb, ALU.mult, [bP], [bP])
            tt("dve", W1[:], bim[:], zib, ALU.mult, [bP], [bP])
            tt("dve", Bbr[:], Bbr[:], W1[:], ALU.subtract, [bP], [bP])
            tt("dve", Bbi[:], bim[:], zrb, ALU.mult, [bP], [bP])
            tt("dve", W1[:], bre[:], zib, ALU.mult, [bP], [bP])
            tt("dve", Bbi[:], Bbi[:], W1[:], ALU.add, [bP], [bP])
            BS = sb(es, "BS", [128, 1024], F32)
            cp("dve", BS[0:64, :], Bbr[0:64].rearrange("p g c -> p (g c)"), [bP], [bP])
            cp("dve", BS[64:128, :], Bbi[64:128].rearrange("p g c -> p (g c)"), [bP], [bP])
            Yre = sb(es, "Yre", [128, 64, 16], F32)
            Yim = sb(es, "Yim", [128, 64, 16], F32)
            CAS = sb(es, "CAS", [128, 1024], F32)
            KBt = sb(es, "KBt", [128, 8, 8, 128], BF16)
            b_KBt = Buf()
            stgc = [sb(es, f"stgc{i}", [128, 4, 2, 128], BF16) for i in range(2)]
            b_stgc = bufs(2)
            ktmp = sb(es, "ktmp", [128, 128], F32)
            sc = 0
            for k in range(9):
                prb = PRt[:, k, :, :].broadcast_to([128, 64, 16])
                pib = PIt[:, k, :, :].broadcast_to([128, 64, 16])
                tt("dve", W1[:], Cre[:], prb, ALU.mult, [bP, bC], [bP])
                tt("dve", W2[:], Cim[:], pib, ALU.mult, [bP, bC], [bP])
                tt("dve", Yre[:], W1[:], W2[:], ALU.subtract, [bP], [bP])
                tt("dve", W1[:], Cre[:], pib, ALU.mult, [bP, bC], [bP])
                tt("dve", W2[:], Cim[:], prb, ALU.mult, [bP, bC], [bP])
                stt(Yim[:].rearrange("p g c -> p (g c)"), W1[:].rearrange("p g c -> p (g c)"), -1.0,
                    W2[:].rearrange("p g c -> p (g c)"), ALU.mult, ALU.subtract, [bP], [bP])
                yref = Yre[:].rearrange("p g c -> p (g c)")
                yimf = Yim[:].rearrange("p g c -> p (g c)")
                if k <= 7:
                    cp("dve", CAS[0:64, :], yref[0:64, :], [bP], [bP])
                    cp("dve", CAS[64:128, :], yimf[64:128, :], [bP], [bP])
                    for ft in range(8):
                        pi = 4 + (ft % 2)
                        S.op("pe", lambda e: e.matmul(PS[pi][:, 0:128], lhsT=BS[:, ft * 128:(ft + 1) * 128],
                                                      rhs=CAS[:, ft * 128:(ft + 1) * 128], start=True, stop=True),
                             r=[bP], w=[PSB[pi]])
                        if k == 0:
                            tt("dve", ktmp[:], PS[pi][:, 0:128], bdm[:], ALU.mult, [PSB[pi], bP], [bP])
                            stt(KBt[:, ft, k, :], ident_f[:], dsk[:, ft:ft + 1], ktmp[:], ALU.mult, ALU.add,
                                [bP, b_const], [b_KBt])
                        else:
                            tt("dve", KBt[:, ft, k, :], PS[pi][:, 0:128], bdm[:], ALU.mult, [PSB[pi], bP], [b_KBt])
                if k >= 1:
                    tau = k - 1
                    for ft in range(8):
                        sl = sc % 2
                        sc += 1
                        for ri, yf in enumerate((yref, yimf)):
                            src = yf[:, ft * 128:(ft + 1) * 128]
                            srcb = xap(src, 0, [[0, 4], [1, 128]])
                            tt("dve", stgc[sl][:, :, ri, :], srcb, cmask[:], ALU.mult, [bP], [b_stgc[sl]])
                        S.dma(CYM[ft, :, tau, :, :, :], stgc[sl][:], r=[b_stgc[sl]], w=[b_CYMd[ft]], eng="pool")
            for ft in range(8):
                S.dma(KBM[ft, :, :, :], KBt[:, ft, :, :], r=[b_KBt], w=[b_KBMd[ft]], eng="pool")
            XR = Yre
            XI = Yim
            for s_ in range(8):
                k = 7 - s_
                prb = PRt[:, k, :, :].broadcast_to([128, 64, 16])
                pib = PIt[:, k, :, :].broadcast_to([128, 64, 16])
                tt("dve", W1[:], Bbr[:], prb, ALU.mult, [bP], [bP])
                tt("dve", W2[:], Bbi[:], pib, ALU.mult, [bP], [bP])
                tt("dve", XR[:], W1[:], W2[:], ALU.subtract, [bP], [bP])
                tt("dve", W1[:], Bbr[:], pib, ALU.mult, [bP], [bP])
                tt("dve", W2[:], Bbi[:], prb, ALU.mult, [bP], [bP])
                tt("dve", XI[:], W1[:], W2[:], ALU.add, [bP], [bP])
                for ft in range(8):
                    sl = sc % 2
                    sc += 1
                    for ri, xf in enumerate((XR, XI)):
                        pi = 6 + ri
                        xin = xf[0:64].rearrange("p g c -> p (g c)")[:, ft * 128:(ft + 1) * 128]
                        S.op("pe", lambda e: e.transpose(out=PS[pi][:, 0:64], in_=xin, identity=ident_f[0:64, 0:64]),
                             r=[bP, b_const], w=[PSB[pi]])
                        srcb = xap(PS[pi][:, 0:64], 0, [[0, 4], [0, 2], [1, 64]])
                        mskb = xap(rm8[:], 0, [[2, 4], [1, 2], [0, 64]])
                        outv = stgc[sl][:, :, ri, :].rearrange("p q (h c) -> p q h c", h=2)
                        tt("dve", outv, srcb, mskb, ALU.mult, [PSB[pi], bP], [b_stgc[sl]])
                    S.dma(WSM[ft, :, s_, :, :, :], stgc[sl][:], r=[b_stgc[sl]], w=[b_WSMd[ft]], eng="pool")

        with PhaseStack(S) as es:
            u2 = [sb(es, f"u2{i}", [128, R_ALL], BF16) for i in range(2)]
            b_u2 = bufs(2)
            wsb = [sb(es, f"wsb{i}", [128, 8, 4, 2, 128], BF16) for i in range(2)]
            b_wsb = bufs(2)
            kbb = [sb(es, f"kbb{i}", [128, 8, 128], BF16) for i in range(2)]
            b_kbb = bufs(2)
            SR = sb(es, "SR", [128, 8, 1040], F32)
            SI = sb(es, "SI", [128, 8, 1040], F32)
            b_HR = bufs(16)
            b_HI = bufs(16)
            ER = [sb(es, f"ER{i}", [128, 8, 65], F32) for i in range(2)]
            EI = [sb(es, f"EI{i}", [128, 8, 65], F32) for i in range(2)]
            b_ER = bufs(2)
            b_EI = bufs(2)
            AdR = [sb(es, f"AdR{i}", [128, 8, 1], F32) for i in range(2)]
            AdI = [sb(es, f"AdI{i}", [128, 8, 1], F32) for i in range(2)]
            b_Ad = bufs(2)
            sq = [sb(es, f"sq{i}", [128, 8, 1], F32) for i in range(3)]
            b_sq = Buf()
            t1 = sb(es, "t1", [128, 8, 65], F32)
            t2 = sb(es, "t2", [128, 8, 65], F32)
            t3 = sb(es, "t3", [128, 8, 65], F32)
            t4 = sb(es, "t4", [128, 8, 65], F32)
            b_t12 = Buf()
            b_t34 = Buf()
            HbR = sb(es, "HbR", [128, 8, NB, 17], BF16)
            HbI = sb(es, "HbI", [128, 8, NB, 17], BF16)
            b_Hb = bufs(2)
            ygst = [sb(es, f"ygst{i}", [128, NB, BW], BF16) for i in range(2)]
            b_ygst = bufs(2)
            g1 = sb(es, "g1", [128, NB * 17], F32)
            g2 = sb(es, "g2", [128, NB * 17], F32)
            b_g = Buf()
            NM = 260
            ecount = 0
            for gbt in range(4):
                Q0 = 8 * gbt
                for ftl in range(2):
                    ft = 2 * gbt + ftl
                    S.dma(u2[ftl][:], UT[ft, :, :], r=b_UTd[ft], w=[b_u2[ftl]])
                    S.dma(wsb[ftl][:], WSM[ft, :, :, :, :, :], r=[b_WSMd[ft]], w=[b_wsb[ftl]])
                    S.dma(kbb[ftl][:], KBM[ft, :, :, :], r=[b_KBMd[ft]], w=[b_kbb[ftl]])
                for ftl in range(2):
                    for qq in range(4):
                        for ri in range(2):
                            dstt = SR if ri == 0 else SI
                            bd = b_HR if ri == 0 else b_HI
                            for mc in range(4):
                                pi = ecount % 4
                                ecount += 1
                                for s_ in range(8):
                                    st = NM * mc * 8 + s_
                                    S.op("pe", lambda e: e.matmul(
                                        PS[pi][:, 0:NM], lhsT=wsb[ftl][:, s_, qq, ri, :],
                                        rhs=u2[ftl][:, st: st + 8 * (NM - 1) + 1: 8], start=(s_ == 0), stop=(s_ == 7)),
                                        r=[b_u2[ftl], b_wsb[ftl]], w=[PSB[pi]])
                                cp("act" if ecount % 2 else "dve", dstt[:, 4 * ftl + qq, NM * mc: NM * (mc + 1)],
                                   PS[pi][:, 0:NM], [PSB[pi]], bd)
                HRv = SR[:].rearrange("p q (M u) -> p q M u", u=16)
                HIv = SI[:].rearrange("p q (M u) -> p q M u", u=16)

                def cmul_acc(curR, curI, aR, aI, pR, pI, rR, rI, wR, wI):
                    n = curR.shape[2]
                    tt("dve", t1[:, :, 0:n], aR, pR, ALU.mult, rR + [b_A], [b_t12])
                    tt("dve", t2[:, :, 0:n], aI, pI, ALU.mult, rI + [b_A], [b_t12])
                    tt("dve", curR, curR, t1[:, :, 0:n], ALU.add, [b_t12], wR)
                    tt("dve", curR, curR, t2[:, :, 0:n], ALU.subtract, [b_t12], wR)
                    tt("dve", t3[:, :, 0:n], aR, pI, ALU.mult, rI + [b_A], [b_t34])
                    tt("dve", t4[:, :, 0:n], aI, pR, ALU.mult, rR + [b_A], [b_t34])
                    tt("dve", curI, curI, t3[:, :, 0:n], ALU.add, [b_t34], wI)
                    tt("dve", curI, curI, t4[:, :, 0:n], ALU.add, [b_t34], wI)

                a1R = AR[:, 0, Q0:Q0 + 8, :].broadcast_to([128, 8, 65])
                a1I = AI[:, 0, Q0:Q0 + 8, :].broadcast_to([128, 8, 65])
                for mu in range(1, 16):
                    cmul_acc(HRv[:, :, :, mu], HIv[:, :, :, mu], a1R, a1I, HRv[:, :, :, mu - 1], HIv[:, :, :, mu - 1],
                             [b_HR[mu - 1]], [b_HI[mu - 1]], [b_HR[mu]], [b_HI[mu]])
                cp("dve", ER[0][:], HRv[:, :, :, 15], [b_HR[15]], [b_ER[0]])
                cp("dve", EI[0][:], HIv[:, :, :, 15], [b_HI[15]], [b_EI[0]])
                cp("dve", AdR[0][:], AR[:, 15, Q0:Q0 + 8, :], [b_A], [b_Ad[0]])
                cp("dve", AdI[0][:], AI[:, 15, Q0:Q0 + 8, :], [b_A], [b_Ad[0]])
                cur = 0
                d = 1
                while d < 65:
                    nxt = 1 - cur
                    cp("dve", ER[nxt][:], ER[cur][:], [b_ER[cur]], [b_ER[nxt]])
                    cp("dve", EI[nxt][:], EI[cur][:], [b_EI[cur]], [b_EI[nxt]])
                    n = 65 - d
                    adR = AdR[cur][:].broadcast_to([128, 8, n])
                    adI = AdI[cur][:].broadcast_to([128, 8, n])
                    tt("dve", t1[:, :, 0:n], adR, ER[cur][:, :, 0:n], ALU.mult, [b_Ad[cur], b_ER[cur]], [b_t12])
                    tt("dve", t2[:, :, 0:n], adI, EI[cur][:, :, 0:n], ALU.mult, [b_Ad[cur], b_EI[cur]], [b_t12])
                    tt("dve", ER[nxt][:, :, d:65], ER[nxt][:, :, d:65], t1[:, :, 0:n], ALU.add, [b_t12], [b_ER[nxt]])
                    tt("dve", ER[nxt][:, :, d:65], ER[nxt][:, :, d:65], t2[:, :, 0:n], ALU.subtract, [b_t12], [b_ER[nxt]])
                    tt("dve", t3[:, :, 0:n], adR, EI[cur][:, :, 0:n], ALU.mult, [b_Ad[cur], b_EI[cur]], [b_t34])
                    tt("dve", t4[:, :, 0:n], adI, ER[cur][:, :, 0:n], ALU.mult, [b_Ad[cur], b_ER[cur]], [b_t34])
                    tt("dve", EI[nxt][:, :, d:65], EI[nxt][:, :, d:65], t3[:, :, 0:n], ALU.add, [b_t34], [b_EI[nxt]])
                    tt("dve", EI[nxt][:, :, d:65], EI[nxt][:, :, d:65], t4[:, :, 0:n], ALU.add, [b_t34], [b_EI[nxt]])
                    tt("dve", sq[0][:], AdR[cur][:], AdR[cur][:], ALU.mult, [b_Ad[cur]], [b_sq])
                    tt("dve", sq[1][:], AdI[cur][:], AdI[cur][:], ALU.mult, [b_Ad[cur]], [b_sq])
                    tt("dve", sq[2][:], AdR[cur][:], AdI[cur][:], ALU.mult, [b_Ad[cur]], [b_sq])
                    tt("dve", AdR[nxt][:], sq[0][:], sq[1][:], ALU.subtract, [b_sq], [b_Ad[nxt]])
                    ts("dve", AdI[nxt][:], sq[2][:], 2.0, None, ALU.mult, None, [b_sq], [b_Ad[nxt]])
                    cur = nxt
                    d *= 2
                for mu in range(15):
                    aR = AR[:, mu, Q0:Q0 + 8, :].broadcast_to([128, 8, 64])
                    aI = AI[:, mu, Q0:Q0 + 8, :].broadcast_to([128, 8, 64])
                    cmul_acc(HRv[:, :, 1:65, mu], HIv[:, :, 1:65, mu], aR, aI, ER[cur][:, :, 0:64], EI[cur][:, :, 0:64],
                             [b_ER[cur]], [b_EI[cur]], [b_HR[mu]], [b_HI[mu]])
                cp("dve", HRv[:, :, :, 15], ER[cur][:], [b_ER[cur]], [b_HR[15]])
                cp("dve", HIv[:, :, :, 15], EI[cur][:], [b_EI[cur]], [b_HI[15]])
                cp("dve", HbR[:], xap(SR[:], 48, [[1040, 8], [64, NB], [1, 17]]), b_HR, [b_Hb[0]])
                cp("dve", HbI[:], xap(SI[:], 48, [[1040, 8], [64, NB], [1, 17]]), b_HI, [b_Hb[1]])
                for ftl in range(2):
                    ft = 2 * gbt + ftl
                    S.dma(wsb[ftl][:], CYM[ft, :, :, :, :, :], r=[b_CYMd[ft]], w=[b_wsb[ftl]])
                for ftl in range(2):
                    ft = 2 * gbt + ftl
                    for tau in range(8):
                        pi = 4 + (tau % 2)
                        po = PS[pi][:, 0:NB * 17]
                        first = True
                        for s_ in range(tau + 1):
                            S.op("pe", lambda e: e.matmul(po, lhsT=kbb[ftl][:, tau - s_, :],
                                                          rhs=xap(u2[ftl][:], 392 + s_, [[512, NB], [8, 17]]),
                                                          start=first, stop=False),
                                 r=[b_u2[ftl], b_kbb[ftl]], w=[PSB[pi]])
                            first = False
                        for qq in range(4):
                            for ri in range(2):
                                hb_ = HbR if ri == 0 else HbI
                                last = (qq == 3 and ri == 1)
                                S.op("pe", lambda e: e.matmul(po, lhsT=wsb[ftl][:, tau, qq, ri, :],
                                                              rhs=hb_[:, 4 * ftl + qq, :, :], start=False, stop=last),
                                     r=[b_Hb[ri], b_wsb[ftl]], w=[PSB[pi]])
                        actf(g1[:], po, AF.Square, [PSB[pi]], [b_g])
                        ts("dve", g1[:], g1[:], 0.044715, 1.0, ALU.mult, ALU.add, [b_g], [b_g])
                        tt("dve", g1[:], g1[:], po, ALU.mult, [b_g, PSB[pi]], [b_g])
                        actf(g2[:], g1[:], AF.Sigmoid, [b_g], [b_g], scale=1.5957691216057308)
                        outv = xap(ygst[ftl][:], tau, [[BW, NB], [8, 17]])
                        tt("dve", outv, g2[:].rearrange("p (i m) -> p i m", m=17), po.rearrange("p (i m) -> p i m", m=17),
                           ALU.mult, [b_g, PSB[pi]], [b_ygst[ftl]])
                    S.dma(YG[ft, :, :], ygst[ftl][:].rearrange("p i q -> p (i q)"), r=[b_ygst[ftl]], w=[b_YGd[ft]], eng="pool")
        es_ssm.close()
        if stage == "C":
            dY = dout("dbgY", [8, 128, R_OWN], BF16)
            final_evs.append(S.dma(dY[:, :, :], YG[:, :, :], r=b_YGd))

    def own_segments(r0, r1):
        segs = []
        r = r0
        while r < r1:
            i, q = divmod(r, BW)
            n = min(r1 - r, BW - q)
            segs.append((r, own_row(i, q), n, i, q))
            r += n
        return segs

    if stage == "full":
        NTO = R_OWN // 128
        H1 = dscr("H1", [R_OWN, D], F32)
        ACTS = dscr("ACTS", [NTO, 128, 44, 128], BF16)
        b_H1d = [bufs(4) for _ in range(NTO)]
        b_ACTSd = bufs(44)
        out_t = dout("out", [NB * 128, D])
        ntiles5 = [(0, 512), (512, 1024), (1024, 1536), (1536, 2048), (2048, 2176)]
        es_mn = PhaseStack(S)
        MN = sb(es_mn, "MN", [128, 16, R_OWN], BF16)
        b_MN = [bufs(2) for _ in range(NTO)]
        G2 = sb(es_mn, "G2", [128, 16], F32)
        b_G2 = Buf()
        S.dma(G2[:], g_ffn.rearrange("(t p) -> p t", p=128), w=[b_G2])
        with PhaseStack(S) as es:
            ygT = sb(es, "ygT", [128, 8, R_OWN], BF16)
            atT = sb(es, "atT", [128, 8, R_OWN], BF16)
            b_yg = Buf()
            b_at = Buf()
            for kt in range(8):
                S.dma(ygT[:, kt, :], YG[kt, :, :], r=[b_YGd[kt]], w=[b_yg])
                S.dma(atT[:, kt, :], AT[kt, :, :], r=[b_ATd[kt]], w=[b_at])
            wst3 = [sb(es, f"w3s{i}", [128, 8, 128], F32) for i in range(3)]
            b_wst3 = bufs(3)
            wb3 = [[sb(es, f"w3b{i}_{j}", [128, 8, 128], BF16) for i in range(3)] for j in range(2)]
            b_wb3 = [bufs(3) for _ in range(2)]
            sgt = [sb(es, f"sg{i}", [128, R_OWN], BF16) for i in range(2)]
            b_sgt = bufs(2)
            sbz = sb(es, "sbz", [128, 512], F32)
            m1 = sb(es, "m1", [128, 512], F32)
            m2 = sb(es, "m2", [128, 512], F32)
            b_m = Buf()
            pc = 0
            for ct in range(16):
                wsl = ct % 2
                srcs = (w_glu[:, ct * 128:(ct + 1) * 128], w_glu[:, 2048 + ct * 128: 2048 + (ct + 1) * 128],
                        w_attn_o[:, ct * 128:(ct + 1) * 128])
                for wi in range(3):
                    S.dma(wst3[wi][:], srcs[wi].rearrange("(t p) c -> p t c", p=128), w=[b_wst3[wi]])
                    cast_plain("act", wb3[wsl][wi][:], wst3[wi][:], [b_wst3[wi]], [b_wb3[wsl][wi]])
                S.dma(sgt[0][:], SGA[ct, :, :], r=[b_SGAd[ct]], w=[b_sgt[0]])
                S.dma(sgt[1][:], SGB[ct, :, :], r=[b_SGBd[ct]], w=[b_sgt[1]])
                for (n0, n1) in ntiles5:
                    N = n1 - n0
                    pis = [(pc + x) % 8 for x in range(3)]
                    pc += 3
                    for wi, act_in, bact in ((0, ygT, b_yg), (1, ygT, b_yg), (2, atT, b_at)):
                        pi = pis[wi]
                        for kt in range(8):
                            S.op("pe", lambda e: e.matmul(PS[pi][:, 0:N], lhsT=wb3[wsl][wi][:, kt, :],
                                                          rhs=act_in[:, kt, n0:n1], start=(kt == 0), stop=(kt == 7)),
                                 r=[b_wb3[wsl][wi], bact], w=[PSB[pi]])
                    actf(sbz[:, 0:N], PS[pis[1]][:, 0:N], AF.Sigmoid, [PSB[pis[1]]], [b_m])
                    tt("dve", m1[:, 0:N], PS[pis[0]][:, 0:N], sbz[:, 0:N], ALU.mult, [PSB[pis[0]], b_m], [b_m])
                    tt("dve", m1[:, 0:N], m1[:, 0:N], sgt[0][:, n0:n1], ALU.mult, [b_m, b_sgt[0]], [b_m])
                    tt("dve", m2[:, 0:N], PS[pis[2]][:, 0:N], sgt[1][:, n0:n1], ALU.mult,
                       [PSB[pis[2]], b_sgt[1]], [b_m])
                    wtok = [b_MN[t][ct // 8] for t in range(n0 // 128, (n1 + 127) // 128)]
                    tt("dve", MN[:, ct, n0:n1], m1[:, 0:N], m2[:, 0:N], ALU.add, [b_m], wtok)
        if DEBUG:
            for nm_, src_, toks_ in (("dbgAT", AT, b_ATd), ("dbgYG", YG, b_YGd), ("dbgSGA", SGA, b_SGAd), ("dbgSGB", SGB, b_SGBd)):
                dd_ = dout(nm_, list(src_.shape), BF16)
                final_evs.append(S.dma(dd_[:, :, :], src_[:, :, :], r=toks_))
            dM = dout("dbgM", [128, 16 * R_OWN], BF16)
            final_evs.append(S.dma(dM[:, :], MN[:].rearrange("p c r -> p (c r)"), r=[b for t in range(NTO) for b in b_MN[t]]))
        with PhaseStack(S) as es:
            Wo = sb(es, "Wo", [128, 16, D], BF16)
            b_Wo = bufs(16)
            wos = [sb(es, f"wos{i}", [128, 16, 128], F32) for i in range(2)]
            b_wos = bufs(2)
            for c8 in range(16):
                sl = c8 % 2
                S.dma(wos[sl][:], w_out[:, c8 * 128:(c8 + 1) * 128].rearrange("(t p) c -> p t c", p=128), w=[b_wos[sl]])
                cast_plain("act" if c8 % 2 else "dve", Wo[:, :, c8 * 128:(c8 + 1) * 128], wos[sl][:], [b_wos[sl]], [b_Wo[c8]])
            xo = [sb(es, "xo4", [128, D], F32)] * 2
            b_xo = [Buf()] * 2
            h1t = [sb(es, f"h1t{i}", [128, D], F32) for i in range(2)]
            b_h1t = bufs(2)
            junk = sb(es, "junk4", [128, D], F32)
            ssq = sb(es, "ssq4", [128, 1], F32)
            ms = sb(es, "ms4", [128, 1], F32)
            rstd = sb(es, "rstd4", [128, 1], F32)
            xs = sb(es, "xs4", [128, D], BF16)
            nb_ = (junk, ssq, ms, rstd, xs, Buf(), Buf())
            for t in range(NTO):
                sl = t % 2
                for (r, hr, n, i_, q_) in own_segments(128 * t, 128 * t + 128):
                    S.dma(xo[sl][r - 128 * t: r - 128 * t + n, :], hall[hr:hr + n, :], w=[b_xo[sl]])
                for cc in range(4):
                    for dt_ in range(16):
                        S.op("pe", lambda e: e.matmul(PS[cc][:, :], lhsT=MN[:, dt_, 128 * t:128 * t + 128],
                                                      rhs=Wo[:, dt_, cc * 512:(cc + 1) * 512], start=(dt_ == 0), stop=(dt_ == 15)),
                             r=b_MN[t] + b_Wo[4 * cc:4 * cc + 4], w=[PSB[cc]])
                    tt("dve", h1t[sl][:, cc * 512:(cc + 1) * 512], PS[cc][:, :], xo[sl][:, cc * 512:(cc + 1) * 512], ALU.add,
                       [PSB[cc], b_xo[sl]], [b_h1t[sl]])
                S.dma(H1[128 * t:128 * t + 128, :], h1t[sl][:], r=[b_h1t[sl]], w=b_H1d[t], eng="pool")
                norm_transpose(nb_, h1t[sl], b_h1t[sl], 128,
                               lambda half, t=t: MN[:, half * 8:(half + 1) * 8, 128 * t:128 * t + 128], b_MN[t], (4, 5))
        if DEBUG:
            dH = dout("dbgH1", [R_OWN, D])
            final_evs.append(S.dma(dH[:, :], H1[:, :], r=[b for t in range(NTO) for b in b_H1d[t]]))
            dN = dout("dbgN2", [128, 16 * R_OWN], BF16)
            final_evs.append(S.dma(dN[:, :], MN[:].rearrange("p c r -> p (c r)"), r=[b for t in range(NTO) for b in b_MN[t]]))
        all_MN = [b for t in range(NTO) for b in b_MN[t]]
        with PhaseStack(S) as es:
            CW = sb(es, "CW", [128, 44, 3], F32)
            CB = sb(es, "CB", [128, 44], F32)
            b_cw = Buf()
            for j3 in range(3):
                S.dma(CW[:, :, j3], conv_w[j3, :].rearrange("(t p) -> p t", p=128), w=[b_cw])
            S.dma(CB[:], conv_b.rearrange("(t p) -> p t", p=128), w=[b_cw])
            wus = [sb(es, f"wus{i}", [128, 16, 256], F32) for i in range(2)]
            b_wus = bufs(2)
            wub = [[sb(es, f"wub{i}_{j}", [128, 16, 256], BF16) for i in range(2)] for j in range(2)]
            b_wub = [bufs(2) for _ in range(2)]
            graw = sb(es, "graw", [128, NB, BW], F32)
            tg = sb(es, "tg", [128, NB, BW], F32)
            ubuf = sb(es, "ubuf", [128, NB, BW], BF16)
            b_graw = Buf()
            b_tg = Buf()
            b_ub = Buf()
            acst = [sb(es, f"acst{i}", [128, R_OWN], BF16) for i in range(2)]
            b_acst = bufs(2)
            grf = graw[:].rearrange("p b q -> p (b q)")
            ubf = ubuf[:].rearrange("p b q -> p (b q)")
            pc = 0
            def wload(cpi_):
                for wi, off in enumerate((cpi_ * 256, DFF + cpi_ * 256)):
                    S.dma(wus[wi][:], w_up[:, off:off + 256].rearrange("(t p) c -> p t c", p=128), w=[b_wus[wi]])

            def wcast(cpi_):
                for wi in range(2):
                    for dt_ in range(16):
                        cast_scaled("act" if wi == 0 else "dve", wub[cpi_ % 2][wi][:, dt_, :], wus[wi][:, dt_, :], G2[:, dt_:dt_ + 1],
                                    [b_wus[wi], b_G2], [b_wub[cpi_ % 2][wi]])

            wload(0)
            wcast(0)
            for cpi in range(22):
                wsl = cpi % 2
                if cpi + 1 < 22:
                    wload(cpi + 1)
                for sub in range(2):
                    if sub == 1 and cpi + 1 < 22:
                        wcast(cpi + 1)
                    c = 2 * cpi + sub
                    asl = c % 2
                    for (n0, n1) in ntiles5:
                        N = n1 - n0
                        pg = pc % 8
                        pu = (pc + 1) % 8
                        pc += 2
                        for wi, pi in ((0, pg), (1, pu)):
                            for dt_ in range(16):
                                S.op("pe", lambda e: e.matmul(PS[pi][:, 0:N], lhsT=wub[wsl][wi][:, dt_, sub * 128:(sub + 1) * 128],
                                                              rhs=MN[:, dt_, n0:n1], start=(dt_ == 0), stop=(dt_ == 15)),
                                     r=all_MN + [b_wub[wsl][wi]], w=[PSB[pi]])
                        S.op("act", lambda e: e.copy(out=grf[:, n0:n1], in_=PS[pg][:, 0:N]), r=[PSB[pg]], w=[b_graw])
                        S.op("act", lambda e: e.copy(out=ubf[:, n0:n1], in_=PS[pu][:, 0:N]), r=[PSB[pu]], w=[b_ub])
                    ts("dve", tg[:], graw[:], CW[:, c, 2:3], CB[:, c:c + 1], ALU.mult, ALU.add, [b_graw, b_cw], [b_tg])
                    stt(tg[:, :, 1:BW], graw[:, :, 0:BW - 1], CW[:, c, 1:2], tg[:, :, 1:BW], ALU.mult, ALU.add,
                        [b_graw, b_cw, b_tg], [b_tg])
                    stt(tg[:, :, 2:BW], graw[:, :, 0:BW - 2], CW[:, c, 0:1], tg[:, :, 2:BW], ALU.mult, ALU.add,
                        [b_graw, b_cw, b_tg], [b_tg])
                    actf(tg[:], tg[:], AF.Silu, [b_tg], [b_tg])
                    tt("dve", acst[asl][:].rearrange("p (b q) -> p b q", q=BW), tg[:], ubuf[:], ALU.mult,
                       [b_tg, b_ub], [b_acst[asl]])
                    S.dma(ACTS[:, :, c, :].rearrange("t p r -> p t r"), acst[asl][:].rearrange("p (t r) -> p t r", r=128),
                          r=[b_acst[asl]], w=[b_ACTSd[c]], eng="pool")
        es_mn.close()
        with PhaseStack(S) as es:
            Wd = [sb(es, f"Wd{i}", [128, 44, 512], BF16) for i in range(2)]
            b_Wd = [bufs(4) for _ in range(2)]
            wds = [sb(es, f"wds{i}", [128, 11, 512], F32) for i in range(2)]
            b_wds = bufs(2)
            aT = [sb(es, f"aT{i}", [128, 44, 128], BF16) for i in range(2)]
            b_aT = bufs(2)
            hc = [sb(es, f"hc{i}", [128, 512], F32) for i in range(2)]
            b_hc = bufs(2)
            kctr = [0]

            def wd_load_cast(cc_):
                for piece in range(4):
                    sl_ = kctr[0] % 2
                    kctr[0] += 1
                    S.dma(wds[sl_][:], w_down[piece * 1408:(piece + 1) * 1408, cc_ * 512:(cc_ + 1) * 512].rearrange(
                        "(t p) c -> p t c", p=128), w=[b_wds[sl_]])
                    cast_plain("act", Wd[cc_ % 2][:, piece * 11:(piece + 1) * 11, :], wds[sl_][:], [b_wds[sl_]],
                               [b_Wd[cc_ % 2][piece]])

            wd_load_cast(0)
            for cc in range(4):
                for t in range(NTO):
                    if t == 2 and cc + 1 < 4:
                        wd_load_cast(cc + 1)
                    sl = t % 2
                    S.dma(aT[sl][:], ACTS[t, :, :, :], r=b_ACTSd, w=[b_aT[sl]])
                    S.dma(hc[sl][:], H1[128 * t:128 * t + 128, cc * 512:(cc + 1) * 512], r=[b_H1d[t][cc]], w=[b_hc[sl]])
                    pi = t % 2
                    for c in range(44):
                        S.op("pe", lambda e: e.matmul(PS[pi][:, :], lhsT=aT[sl][:, c, :], rhs=Wd[cc % 2][:, c, :],
                                                      start=(c == 0), stop=(c == 43)),
                             r=[b_aT[sl], b_Wd[cc % 2][c // 11]], w=[PSB[pi]])
                    tt("dve", hc[sl][:], hc[sl][:], PS[pi][:, :], ALU.add, [PSB[pi], b_hc[sl]], [b_hc[sl]])
                    S.dma(H1[128 * t:128 * t + 128, cc * 512:(cc + 1) * 512], hc[sl][:], r=[b_hc[sl]], w=[b_H1d[t][cc]], eng="pool")
        with PhaseStack(S) as es:
            GF = sb(es, "GF", [128, D], F32)
            b_GF = Buf()
            S.dma(GF[:], g_final.partition_broadcast(128), w=[b_GF])
            hf_ = [sb(es, f"hf{i}", [128, D], F32) for i in range(2)]
            b_hf = bufs(2)
            of_ = [sb(es, f"of{i}", [128, D], F32) for i in range(2)]
            b_of = bufs(2)
            junk = sb(es, "junk6", [128, D], F32)
            ssq = sb(es, "ssq6", [128, 1], F32)
            ms = sb(es, "ms6", [128, 1], F32)
            rstd = sb(es, "rstd6", [128, 1], F32)
            b_n = Buf()
            for t in range(NTO):
                sl = t % 2
                S.dma(hf_[sl][:], H1[128 * t:128 * t + 128, :], r=b_H1d[t], w=[b_hf[sl]])
                S.op("act", lambda e: e.activation(out=junk[:], in_=hf_[sl][:], func=AF.Square), r=[b_hf[sl]], w=[b_n])
                w_ = D // 2
                while w_ >= 1:
                    S.op("dve", lambda e, w_=w_: e.tensor_tensor(out=junk[:, 0:w_], in0=junk[:, 0:w_], in1=junk[:, w_:2 * w_],
                                                                 op=ALU.add), r=[b_n], w=[b_n])
                    w_ //= 2
                ssq = junk
                actf(ms[:], ssq[:, 0:1], AF.Sqrt, [b_n, b_const], [b_n], scale=1.0 / D, bias=cst[:, 0:1])
                S.op("dve", lambda e: e.reciprocal(out=rstd[:], in_=ms[:]), r=[b_n], w=[b_n])
                stt(of_[sl][:], hf_[sl][:], rstd[:, 0:1], GF[:], ALU.mult, ALU.mult, [b_n, b_hf[sl], b_GF], [b_of[sl]])
                for (r, hr, n, i_, q_) in own_segments(128 * t, 128 * t + 128):
                    qa = max(q_, 8)
                    if qa >= q_ + n:
                        continue
                    la = r + (qa - q_) - 128 * t
                    cnt_ = q_ + n - qa
                    ev = S.dma(out_t[i_ * 128 + qa - 8: i_ * 128 + qa - 8 + cnt_, :], of_[sl][la:la + cnt_, :], r=[b_of[sl]], eng="pool")
                    final_evs.append(ev)

    S.finish(final_evs)
    es_top.close()
    return nc, outs, S


def host_constants():
    ident = np.eye(128, dtype=np.float32)
    tri = np.triu(np.ones((128, 128), np.float32))
    sel72 = np.zeros((128, 128), np.float32)
    sel72[72, :] = 1.0
    k = np.arange(128)[:, None]
    q = np.arange(BW)[None, :]
    m0 = np.where(k <= q + 8, 0.0, NEG).astype(np.float32)
    m1 = np.where(k + 128 <= q + 8, 0.0, NEG).astype(np.float32)
    KS = list(range(9)) + [8 * m for m in range(2, 17)]
    kkv = np.asarray(KS, np.float32)
    ch = np.arange(128) // 16
    rowmask8 = (ch[:, None] == np.arange(8)[None, :]).astype(np.float32)
    c2 = np.arange(128) // 64
    cm = np.zeros((128, 4, 128), np.float32)
    for qq in range(4):
        for p in range(128):
            g8 = 2 * qq + c2[p]
            cm[p, qq, g8 * 16:(g8 + 1) * 16] = 1.0
    bd = (ch[:, None] == ch[None, :]).astype(np.float32)
    return dict(c_ident=ident, c_tri=tri, c_sel72=sel72, c_mask=np.concatenate([m0, m1], axis=1),
                c_kk=kkv, c_rowmask8=rowmask8, c_cmask=cm.reshape(128, 512), c_bdmask=bd)


def make_in_maps(inputs):
    x = np.asarray(inputs["x"], np.float32)
    meta = np.asarray(inputs["meta"], np.float32)
    consts = host_constants()
    maps = []
    for c in range(8):
        b, j = divmod(c, 4)
        pad = 128 * (3 - j)
        hall = np.zeros((R_ALL, D), np.float32)
        seq = np.concatenate([meta, x[b]], axis=0)
        n = min(R_ALL - pad, seq.shape[0])
        hall[pad:pad + n] = seq[:n]
        kb = np.zeros((R_ALL,), np.float32)
        kb[:pad] = NEG
        m = dict(hall=hall, kbneg=np.ascontiguousarray(kb.reshape(NT_ALL, 128).T),
                 w_in=np.ascontiguousarray(inputs["w_in"][0], dtype=np.float32),
                 g_mix=np.ascontiguousarray(inputs["g_mix"][0], dtype=np.float32),
                 b_f=np.ascontiguousarray(inputs["b_f"][0], dtype=np.float32))
        for nm in ("lam_re", "lam_im", "log_dt", "b_re", "b_im", "c_re", "c_im", "d_skip", "w_glu", "w_attn_o",
                   "w_out", "g_ffn", "w_up", "conv_w", "conv_b", "w_down"):
            m[nm] = np.ascontiguousarray(inputs[nm][0], dtype=np.float32)
        m["g_final"] = np.ascontiguousarray(inputs["g_final"], dtype=np.float32)
        m.update(consts)
        maps.append(m)
    return maps


_CACHE = {}


def kernel(**inputs):
    if "nc" not in _CACHE:
        _CACHE["nc"] = build("full")[0]
    nc = _CACHE["nc"]
    maps = make_in_maps(inputs)
    res = run_bass_kernel_spmd(nc, maps, core_ids=list(range(8)))
    _CACHE["res"] = res
    out = np.zeros((2, 8192, D), np.float32)
    for c in range(8):
        b, j = divmod(c, 4)
        o = np.asarray(res.results[c]["out"], dtype=np.float32).reshape(NB, 128, D)
        for i in range(NB):
            G = 4 * i + j
            out[b, 128 * G:128 * (G + 1)] = o[i]
    return out
```
